# Optimizing a Trainium2 kernel written in Bass

```python
import jax
import jax.numpy as jnp
from jax import lax
import numpy as np

D_MODEL = 2048
BATCH = 8
SEQ = 2048
DEPTH = 1

D_MIX = D_MODEL
D_RWKV = D_MIX // 2
RWKV_HEAD = 64
RWKV_HEADS = D_RWKV // RWKV_HEAD
D_POOL = D_MIX - D_RWKV
POOL_WINDOWS = (2, 4, 8, 16)
N_POOL_GROUPS = len(POOL_WINDOWS)
POOL_GROUP = D_POOL // N_POOL_GROUPS
DECAY_LORA = max(32, int(round(1.8 * D_RWKV ** 0.5 / 32)) * 32)
AAA_LORA = max(32, int(round(1.8 * D_RWKV ** 0.5 / 32)) * 32)
GATE_LORA = max(32, int(round(0.6 * D_RWKV ** 0.8 / 32)) * 32)
RWKV_SPLITS = (D_RWKV, 2 * D_RWKV, 3 * D_RWKV, 3 * D_RWKV + DECAY_LORA, 3 * D_RWKV + DECAY_LORA + AAA_LORA)
N_RWKV_COLS = 3 * D_RWKV + DECAY_LORA + AAA_LORA + GATE_LORA
D_IN = N_RWKV_COLS + D_POOL
D_FF = ((8 * D_MODEL // 3 + 255) // 256) * 256
CONV_WIDTH = 3
N_MOD = 6
RMS_EPS = 1e-6
GN_EPS = 64e-5
L2_EPS = 1e-12

kernel_name = 'hymba_rwkv7_pool_convffn_adaln'


def _rmsnorm(x, g):
    xf = x.astype(jnp.float32)
    y = xf * lax.rsqrt(jnp.mean(xf * xf, axis=-1, keepdims=True) + RMS_EPS)
    return (y * g.astype(jnp.float32)).astype(x.dtype)


def _token_shift(z):
    return jnp.pad(z, ((0, 0), (1, 0), (0, 0)))[:, :-1]


def _wkv7_scan(r, w, k, v, a, b):
    B, T, H, N = r.shape

    def step(S, inp):
        r_t, w_t, k_t, v_t, a_t, b_t = inp
        sa = jnp.einsum('bhvk,bhk->bhv', S, a_t)
        S = S * w_t[:, :, None, :] + sa[..., None] * b_t[:, :, None, :] + v_t[..., None] * k_t[:, :, None, :]
        y = jnp.einsum('bhvk,bhk->bhv', S, r_t)
        return S, y

    xs = (jnp.moveaxis(r, 1, 0), jnp.moveaxis(w, 1, 0), jnp.moveaxis(k, 1, 0),
          jnp.moveaxis(v, 1, 0), jnp.moveaxis(a, 1, 0), jnp.moveaxis(b, 1, 0))
    S0 = jnp.zeros((B, H, N, N), jnp.float32)
    _, ys = lax.scan(step, S0, xs)
    return jnp.moveaxis(ys, 0, 1)


def _rwkv7_mixer(p, mu, w0, w2, a0, a2, g2, k_k, k_a, r_k, ln_w, ln_b):
    B, T, _ = p.shape
    f32 = jnp.float32
    p = p + mu * (_token_shift(p) - p)
    r, k, v, xw, xa, xg = jnp.split(p, list(RWKV_SPLITS), axis=-1)
    w = -jax.nn.softplus(-(w0 + jnp.tanh(xw) @ w2)) - 0.5
    a = jax.nn.sigmoid(a0 + xa @ a2)
    g = jax.nn.sigmoid(xg) @ g2

    def heads(z):
        return z.reshape(B, T, RWKV_HEADS, RWKV_HEAD)

    kk = heads(k * k_k).astype(f32)
    kk = kk / jnp.maximum(jnp.sqrt(jnp.sum(kk * kk, axis=-1, keepdims=True)), L2_EPS)
    k = k * (1.0 + (a - 1.0) * k_a)
    r_h, k_h, v_h, a_h = heads(r), heads(k), heads(v), heads(a)
    decay = jnp.exp(-jnp.exp(heads(w).astype(f32)))
    y = _wkv7_scan(r_h.astype(f32), decay, k_h.astype(f32), v_h.astype(f32),
                   -kk, kk * a_h.astype(f32))
    mean = jnp.mean(y, axis=-1, keepdims=True)
    var = jnp.mean(jnp.square(y - mean), axis=-1, keepdims=True)
    y = ((y - mean) * lax.rsqrt(var + GN_EPS)).reshape(B, T, D_RWKV)
    y = (y * ln_w.astype(f32) + ln_b.astype(f32)).astype(p.dtype)
    bonus = jnp.sum(r_h * k_h * r_k, axis=-1, keepdims=True) * v_h
    return (y + bonus.reshape(B, T, D_RWKV)) * g


def _pool_mixer(u, pool_w, pool_b, pool_scale):
    B, T, _ = u.shape
    uf = u.astype(jnp.float32).reshape(B, T, N_POOL_GROUPS, POOL_GROUP)
    cs = jnp.cumsum(uf, axis=1)
    pos = jnp.arange(1, T + 1, dtype=jnp.float32)
    outs = []
    for gi, win in enumerate(POOL_WINDOWS):
        c_g = cs[:, :, gi]
        lag = jnp.pad(c_g, ((0, 0), (win, 0), (0, 0)))[:, :T]
        mean = (c_g - lag) / jnp.minimum(pos, win)[None, :, None]
        outs.append(mean - uf[:, :, gi])
    z = jnp.stack(outs, axis=2).astype(u.dtype)
    z = jnp.einsum('btgc,gcd->btgd', z, pool_w) + pool_b
    return z.reshape(B, T, D_POOL) * pool_scale


def _conv_ffn(h, w_up, conv_w, conv_b, w_down):
    T = h.shape[1]
    u = h @ w_up
    up = jnp.pad(u, ((0, 0), (CONV_WIDTH - 1, 0), (0, 0)))
    y = conv_b
    for j in range(CONV_WIDTH):
        y = y + conv_w[j] * up[:, j:j + T]
    val, gate = jnp.split(y, 2, axis=-1)
    return (jax.nn.silu(gate) * val) @ w_down


def setup_inputs(seed: int = 0) -> dict:
    key = jax.random.key(seed)
    ks = jax.random.split(key, 27)
    L = DEPTH

    def nrm(k, shape, scale):
        return jax.random.normal(k, shape, jnp.float32) * scale

    ratio = jnp.arange(D_RWKV, dtype=jnp.float32) / (D_RWKV - 1)
    w0 = (-7.0 + 5.0 * ratio ** 0.85 + 0.5)[None, :] + nrm(ks[5], (L, D_RWKV), 0.1)
    return {
        'x': nrm(ks[0], (BATCH, SEQ, D_MODEL), 1.0),
        'c': nrm(ks[1], (BATCH, D_MODEL), 1.0),
        'w_ada': nrm(ks[2], (L, D_MODEL, N_MOD * D_MODEL), D_MODEL ** -0.5),
        'b_ada': nrm(ks[3], (L, N_MOD * D_MODEL), 0.01),
        'norm1_g': 1.0 + nrm(ks[4], (L, D_MODEL), 0.05),
        'w_in': nrm(ks[6], (L, D_MODEL, D_IN), D_MODEL ** -0.5),
        'mu_shift': jax.random.uniform(ks[7], (L, N_RWKV_COLS), jnp.float32),
        'w0': w0,
        'w2': nrm(ks[8], (L, DECAY_LORA, D_RWKV), 0.5 * DECAY_LORA ** -0.5),
        'a0': nrm(ks[9], (L, D_RWKV), 0.1),
        'a2': nrm(ks[10], (L, AAA_LORA, D_RWKV), AAA_LORA ** -0.5),
        'g2': nrm(ks[11], (L, GATE_LORA, D_RWKV), GATE_LORA ** -0.5),
        'k_k': 0.85 + nrm(ks[12], (L, D_RWKV), 0.05),
        'k_a': 1.0 + nrm(ks[13], (L, D_RWKV), 0.05),
        'r_k': nrm(ks[14], (L, RWKV_HEADS, RWKV_HEAD), 0.1),
        'ln_x_w': 1.0 + nrm(ks[15], (L, D_RWKV), 0.05),
        'ln_x_b': nrm(ks[16], (L, D_RWKV), 0.01),
        'pool_w': nrm(ks[17], (L, N_POOL_GROUPS, POOL_GROUP, POOL_GROUP), POOL_GROUP ** -0.5),
        'pool_b': nrm(ks[18], (L, N_POOL_GROUPS, POOL_GROUP), 0.01),
        'pool_scale': 1.0 + nrm(ks[19], (L, D_POOL), 0.05),
        'w_out': nrm(ks[20], (L, D_MIX, D_MODEL), D_MIX ** -0.5),
        'norm2_g': 1.0 + nrm(ks[21], (L, D_MODEL), 0.05),
        'w_up': nrm(ks[22], (L, D_MODEL, 2 * D_FF), D_MODEL ** -0.5),
        'conv_w': nrm(ks[23], (L, CONV_WIDTH, 2 * D_FF), CONV_WIDTH ** -0.5),
        'conv_b': nrm(ks[24], (L, 2 * D_FF), 0.01),
        'w_down': nrm(ks[25], (L, D_FF, D_MODEL), D_FF ** -0.5),
        'norm_f_g': 1.0 + nrm(ks[26], (D_MODEL,), 0.05),
    }


def reference(x, c, w_ada, b_ada, norm1_g, w_in, mu_shift, w0, w2, a0, a2, g2,
              k_k, k_a, r_k, ln_x_w, ln_x_b, pool_w, pool_b, pool_scale, w_out,
              norm2_g, w_up, conv_w, conv_b, w_down, norm_f_g):
    c_act = jax.nn.silu(c)
    for l in range(DEPTH):
        mod = c_act @ w_ada[l] + b_ada[l]
        sh1, sc1, gt1, sh2, sc2, gt2 = jnp.split(mod[:, None, :], N_MOD, axis=-1)
        h = _rmsnorm(x, norm1_g[l]) * (1.0 + sc1) + sh1
        p = h @ w_in[l]
        y_rwkv = _rwkv7_mixer(p[..., :N_RWKV_COLS], mu_shift[l], w0[l], w2[l], a0[l], a2[l],
                              g2[l], k_k[l], k_a[l], r_k[l], ln_x_w[l], ln_x_b[l])
        y_pool = _pool_mixer(p[..., N_RWKV_COLS:], pool_w[l], pool_b[l], pool_scale[l])
        mix = jnp.concatenate([y_rwkv, y_pool], axis=-1) @ w_out[l]
        x = x + gt1 * mix
        h = _rmsnorm(x, norm2_g[l]) * (1.0 + sc2) + sh2
        x = x + gt2 * _conv_ffn(h, w_up[l], conv_w[l], conv_b[l], w_down[l])
    return _rmsnorm(x, norm_f_g)
```

```python
import numpy as np
import concourse.bass as bass
import concourse.mybir as mybir

F32 = mybir.dt.float32
BF16 = mybir.dt.bfloat16
AF = mybir.ActivationFunctionType
ALU = mybir.AluOpType


class Prog:
    ENGS = ['pe', 'act', 'dve', 'pool', 'sp']

    def __init__(self, nc, es):
        self.nc = nc
        self.es = es
        self.sems = {}
        self.cnt = {e: 0 for e in ['pe', 'act', 'dve', 'pool']}
        self.instrs = {e: [] for e in self.ENGS}
        self.known = {e: {} for e in self.ENGS}
        self.last_w = {}
        self.readers = {}
        self.dcount = {}
        self.fence = {}

    def fence_arena(self):
        f = {}
        for d in (self.last_w, self.readers):
            for key in [k for k in d if k.startswith('A:')]:
                v = d.pop(key)
                items = [v] if isinstance(v, tuple) else list(v.items())
                for k2, v2 in items:
                    if f.get(k2, 0) < v2:
                        f[k2] = v2
        self.fence = f

    def sem(self, key):
        if key not in self.sems:
            self.sems[key] = self.es.enter_context(self.nc.semaphore(key))
        return self.sems[key]

    def _deps(self, eng, reads, writes):
        deps = {}

        def add(tok):
            if tok is None:
                return
            k, v = tok
            if deps.get(k, 0) < v:
                deps[k] = v
        for r in reads:
            add(self.last_w.get(r))
        if any(w.startswith('A:') for w in writes):
            for k, v in self.fence.items():
                add((k, v))
        for w in writes:
            add(self.last_w.get(w))
            for k, v in self.readers.get(w, {}).items():
                add((k, v))
        waits = []
        for k, v in deps.items():
            if eng == 'pe' and k == 'c_pe':
                continue
            if self.known[eng].get(k, 0) >= v:
                continue
            self.known[eng][k] = v
            waits.append((k, v))
        return waits

    def _commit(self, tok, reads, writes):
        for r in reads:
            d = self.readers.setdefault(r, {})
            if d.get(tok[0], 0) < tok[1]:
                d[tok[0]] = tok[1]
        for w in writes:
            self.last_w[w] = tok
            self.readers[w] = {}

    def op(self, eng, fn, reads=(), writes=()):
        waits = self._deps(eng, reads, writes)
        self.cnt[eng] += 1
        tok = ('c_' + eng, self.cnt[eng])
        self.sem(tok[0])
        self.instrs[eng].append((waits, fn, (tok[0], 1)))
        self._commit(tok, reads, writes)
        return tok

    def dma(self, q, out, in_, slot, reads=(), writes=(), **kw):
        waits = self._deps(q, reads, writes)
        key = 'd_' + slot
        self.sem(key)
        self.dcount[key] = self.dcount.get(key, 0) + 16
        tok = (key, self.dcount[key])
        self.instrs[q].append((waits, lambda e: e.dma_start(out=out, in_=in_, **kw), (key, 16)))
        self._commit(tok, reads, writes)
        return tok

    def settle(self, slot, regions):
        key = 'd_' + slot
        tok = (key, self.dcount[key])
        for r in regions:
            self.last_w[r] = tok

    def final_wait(self, eng, regions):
        waits = self._deps(eng, regions, ())
        self.instrs[eng].append((waits, None, None))

    def emit(self):
        nc = self.nc
        with nc.Block() as block:
            decos = {'pe': block.tensor, 'act': block.scalar, 'dve': block.vector,
                     'pool': block.gpsimd, 'sp': block.sync}
            for ename in self.ENGS:
                def make(ename):
                    def body(e):
                        for waits, fn, inc in self.instrs[ename]:
                            for k, v in waits:
                                e.wait_ge(self.sems[k], v)
                            if fn is not None:
                                ins = fn(e)
                                ins.then_inc(self.sems[inc[0]], inc[1])
                    return body
                decos[ename](make(ename))

from contextlib import ExitStack
from concourse.bass_utils import run_bass_kernel_spmd

D = 2048; T = 2048; TT = 512; NCH = 16
DR = 1024; NPAIR = 8
DFF = 5632; NFF = 44
C0 = 0.6065306597126334
RMS_EPS = 1e-6; GN_EPS = 64e-5

CP_FIELDS = [('ccol', 16), ('bada', 96), ('n1g', 16), ('n2g', 16), ('nfg', 16), ('mu', 27),
             ('w0', 8), ('a0', 8), ('kk', 8), ('ka', 8), ('rk', 8), ('lnw', 8), ('lnb', 8),
             ('poolb', 8), ('pools', 8), ('convw', 264), ('convb', 88), ('ident', 128),
             ('onesblk', 128), ('onesblkm', 128), ('amask', 512), ('rmask', 512), ('invd', 64)]
CP_OFF = {}
_o = 0
for _n, _w in CP_FIELDS:
    CP_OFF[_n] = (_o, _w); _o += _w
CP_W = _o


class _Stop(Exception):
    pass


def build_nc(NT=4, stage=99):
    nc = bass.Bass("TRN2", target_bir_lowering=False)
    es = ExitStack()
    with es:
        P = Prog(nc, es)
        din = lambda n, s: nc.dram_tensor(n, s, F32, kind="ExternalInput").ap()
        xT = din("xT", [D, T]).rearrange("(c p) t -> p c t", p=128)
        cpack_d = din("cpack", [128, CP_W])
        wada_d = din("wada", [128, 96, 16, 128])
        win_d = din("win", [128, 35, 16, 128])
        w2a2_d = din("w2a2", [128, 2, 1024])
        g2_d = din("g2", [128, 2, 1024])
        poolw_d = din("poolw", [128, 4, 2, 256])
        wout_d = din("wout", [128, 16, 16, 128])
        wup_d = din("wup", [128, 88, 16, 128])
        wdn_d = din("wdn", [128, 16, 44, 128])
        outT = nc.dram_tensor("outT", [D, T], F32, kind="ExternalOutput").ap().rearrange("(c p) t -> p c t", p=128)

        sb = lambda n, s, d: es.enter_context(nc.sbuf_tensor(n, s, d))
        psb = lambda n: es.enter_context(nc.psum_tensor(n, [128, 512], F32))

        cp = sb("cp", [128, CP_W], F32)
        def cv(name, a=None, b=None):
            o, w = CP_OFF[name]
            if a is None:
                return cp[:, o:o + w]
            return cp[:, o + a:o + (a + 1 if b is None else b)]
        xs = sb("xs", [128, 16, 512], F32)
        h = sb("h", [128, 16, 512], BF16)
        mix = sb("mix", [128, 16, 512], BF16)
        modsb = sb("modsb", [128, 96], F32)
        gs1 = sb("gs1", [128, 16], F32); gs2 = sb("gs2", [128, 16], F32)
        omka = sb("omka", [128, 8], F32)
        cact = sb("cact", [128, 16], BF16)
        ones_bf = sb("ones_bf", [128, 128], BF16)
        epsc = sb("epsc", [128, 2], F32)
        w2a2 = sb("w2a2s", [128, 2, 1024], BF16)
        g2s = sb("g2s", [128, 2, 1024], BF16)
        poolw = sb("poolws", [128, 4, 2, 256], BF16)
        Hst = sb("Hst", [128, 8, 128], F32)
        carry = sb("carry", [128, 27], F32)
        carryp = sb("carryp", [128, 8, 16], F32)
        carryf = sb("carryf", [128, 88, 2], F32)
        NSLOT = 3
        slots = [sb("wslot%d" % i, [128, 6144], BF16) for i in range(NSLOT)]
        ARENA = 20800
        arena = sb("arena", [128, ARENA], F32)
        banks = {n: psb(n) for n in ['acc0', 'acc1', 'st', 'pA', 'pD', 'pT', 'pW', 'pY']}

        class Arena:
            def __init__(self): self.off = 0
            def reset(self): self.off = 0
            def f32(self, n):
                v = arena[:, self.off:self.off + n]; self.off += n
                assert self.off <= ARENA, self.off
                return v
            def bf16(self, n):
                assert n % 2 == 0
                v = arena[:, self.off:self.off + n // 2].bitcast(BF16); self.off += n // 2
                assert self.off <= ARENA, self.off
                return v
        AR = Arena()

        def MM(out, lhsT, rhs, start, stop, r, w):
            P.op('pe', lambda e: e.matmul(out, lhsT=lhsT, rhs=rhs, start=start, stop=stop), r, w)
        def TR(out, in_, r, w):
            idt = cv('ident')
            P.op('pe', lambda e: e.transpose(out, in_, idt), list(r) + ['cp'], w)
        def ACT(out, in_, func, r, w, bias=None, scale=None):
            kw = {}
            if bias is not None: kw['bias'] = bias
            if scale is not None: kw['scale'] = scale
            P.op('act', lambda e: e.activation(out, in_, func, **kw), r, w)
        def TTo(eng, out, in0, in1, op, r, w):
            P.op(eng, lambda e: e.tensor_tensor(out, in0, in1, op), r, w)
        def TS(eng, out, in0, s1, s2, op0, op1, r, w):
            if s2 is None:
                P.op(eng, lambda e: e.tensor_scalar(out, in0, s1, None, op0), r, w)
            else:
                P.op(eng, lambda e: e.tensor_scalar(out, in0, s1, s2, op0, op1), r, w)
        def STT(eng, out, in0, scalar, in1, op0, op1, r, w):
            P.op(eng, lambda e: e.scalar_tensor_tensor(out, in0, scalar, in1, op0, op1), r, w)
        def CPY(eng, out, in_, r, w):
            if eng == 'act':
                P.op('act', lambda e: e.copy(out, in_), r, w)
            else:
                P.op(eng, lambda e: e.tensor_copy(out, in_), r, w)
        def MEMSET(eng, ap, val, w):
            P.op(eng, lambda e: e.memset(ap, val), (), w)
        def v3(ap):
            return ap.rearrange("p (a b) -> p a b", b=64)

        plan = []
        _fl = lambda ap: ap.rearrange("p a k n -> p (a k n)")
        for g in range(32):
            plan.append((_fl(wada_d[:, 3 * g:3 * g + 3]), 6144))
        for _it in range(NT):
            plan.append((_fl(win_d[:, 0:3]), 6144))
            for c in range(8):
                plan.append((_fl(win_d[:, 3 + 3 * c:6 + 3 * c]), 6144))
            for g3 in range(3):
                nb = 3 if g3 < 2 else 2
                plan.append((_fl(win_d[:, 27 + 3 * g3:27 + 3 * g3 + nb]), nb * 2048))
            for g3 in range(6):
                nb = 3 if g3 < 5 else 1
                plan.append((_fl(wout_d[:, 3 * g3:3 * g3 + nb]), nb * 2048))
            for j in range(NFF):
                plan.append((_fl(wup_d[:, 2 * j:2 * j + 2]), 4096))
            for blk in range(16):
                plan.append((wdn_d[:, blk].rearrange("p k n -> p (k n)"), NFF * 128))
        st = {'cur': 0, 'issued': 0}
        PF = 2
        def wload(src, nelem):
            k = st['cur']; st['cur'] += 1
            while st['issued'] <= min(k + PF, len(plan) - 1):
                i = st['issued']; st['issued'] += 1
                psrc, pn = plan[i]
                key = 'ws%d' % (i % NSLOT)
                P.dma('pool', slots[i % NSLOT][:, 0:pn], psrc, key, reads=(), writes=[key])
            assert plan[k][1] == nelem, (k, plan[k][1], nelem)
            return slots[k % NSLOT][:, 0:nelem], 'ws%d' % (k % NSLOT)

        P.dma('sp', cp[:], cpack_d, 'c', writes=['cp'])
        P.dma('pool', w2a2[:], w2a2_d, 'cw', writes=['w2a2'])
        P.dma('pool', g2s[:], g2_d, 'cw', writes=['g2s'])
        P.dma('pool', poolw[:], poolw_d, 'cw', writes=['poolw'])
        P.settle('cw', ['w2a2', 'g2s', 'poolw'])
        MEMSET('pool', ones_bf[:], 1.0, ['ones_bf'])
        MEMSET('pool', epsc[:, 0:1], RMS_EPS, ['epsc'])
        MEMSET('pool', epsc[:, 1:2], GN_EPS, ['epsc'])
        MEMSET('pool', Hst[:], 0.0, ['Hst%d' % c for c in range(8)])
        MEMSET('pool', carry[:], 0.0, ['carry'])
        MEMSET('pool', carryp[:], 0.0, ['carryp'])
        MEMSET('pool', carryf[:], 0.0, ['carryf'])
        ACT(cact[:], cv('ccol'), AF.Silu, ['cp'], ['cact'])
        TS('dve', omka[:], cv('ka'), -1.0, 1.0, ALU.mult, ALU.add, ['cp'], ['omka'])
        modp = banks['st']
        for g in range(32):
            wv, wk = wload(wada_d[:, 3 * g:3 * g + 3].rearrange("p a k n -> p (a k n)"), 6144)
            wv4 = wv.rearrange("p (a k n) -> p a k n", a=3, k=16)
            for a in range(3):
                j = 3 * g + a
                for kc in range(16):
                    MM(modp[:, j:j + 1], wv4[:, a, kc, :], cact[:, kc:kc + 1], kc == 0, kc == 15,
                       [wk, 'cact'], ['st'])
        TTo('dve', modsb[:], modp[:, 0:96], cv('bada'), ALU.add, ['st', 'cp'], ['modsb'])
        sh1 = modsb[:, 0:16]; sc1 = modsb[:, 16:32]; gt1 = modsb[:, 32:48]
        sh2 = modsb[:, 48:64]; sc2 = modsb[:, 64:80]; gt2 = modsb[:, 80:96]
        STT('dve', gs1[:], sc1, 1.0, cv('n1g'), ALU.add, ALU.mult, ['modsb', 'cp'], ['gs1'])
        STT('dve', gs2[:], sc2, 1.0, cv('n2g'), ALU.add, ALU.mult, ['modsb', 'cp'], ['gs2'])

        def rms_stats(tmp_sq, rstd):
            ss = banks['st']
            for c in range(16):
                sq = tmp_sq[c % 2]
                ACT(sq, xs[:, c, :], AF.Square, ['xs%d' % c], ['A:sq%d' % (c % 2)])
                MM(ss[:], ones_bf[:], sq, c == 0, c == 15, ['ones_bf', 'A:sq%d' % (c % 2)], ['st'])
            ACT(rstd, ss[:], AF.Sqrt, ['st', 'epsc'], ['A:rstd'], bias=epsc[:, 0:1], scale=1.0 / D)
            P.op('dve', lambda e: e.reciprocal(rstd, rstd), ['A:rstd'], ['A:rstd'])

        def norm_to_h(gs, sh, gkey):
            AR.reset()
            tmp_sq = [AR.bf16(512), AR.bf16(512)]
            rstd = AR.f32(512)
            tmp = [AR.f32(512), AR.f32(512)]
            rms_stats(tmp_sq, rstd)
            for c in range(16):
                t = tmp[c % 2]
                TTo('dve' if c % 2 == 0 else 'pool', t, xs[:, c, :], rstd, ALU.mult,
                    ['xs%d' % c, 'A:rstd'], ['A:nt%d' % (c % 2)])
                ACT(h[:, c, :], t, AF.Identity, ['A:nt%d' % (c % 2), gkey, 'modsb'], ['h%d' % c],
                    bias=sh[:, c:c + 1], scale=gs[:, c:c + 1])

        hkeys = ['h%d' % c for c in range(16)]
        mixkeys = ['mix%d' % c for c in range(16)]
        accs = ['acc0', 'acc1']
        accn = {'n': 0}
        def next_acc(names=accs):
            n = names[accn['n'] % len(names)]; accn['n'] += 1
            return n

        outkeys = []
        def ck(n):
            if stage == n:
                raise _Stop()
        def dump(ap, slot, w=512):
            P.dma('sp', outT[:, slot, 0:w], ap, 'dbg', reads=['cp', 'modsb', 'gs1'] + ['A:tA', 'A:d', 'A:kk', 'A:k', 'A:r', 'A:v', 'A:s', 'A:a', 'A:g'] * (slot >= 4), writes=['dbg%d' % slot])
            outkeys.append('dbg%d' % slot)
        try:
          ck(0)
          for it in range(NT):
              t0 = it * TT
              for q in range(4):
                  P.dma('sp', xs[:, 4 * q:4 * q + 4, :], xT[:, 4 * q:4 * q + 4, t0:t0 + TT], 'x',
                        writes=['xs%d' % c for c in range(4 * q, 4 * q + 4)])
              P.settle('x', ['xs%d' % c for c in range(16)])
              P.fence_arena()
              norm_to_h(gs1, sh1, 'gs1')
              ck(1)
              P.fence_arena()

              AR.reset()
              praw = AR.f32(528)
              txa = AR.bf16(512); sxg0 = AR.bf16(512); sxg1 = AR.bf16(512)
              r32 = AR.f32(512); k32 = AR.f32(512); v32 = AR.f32(512); a32 = AR.f32(512)
              s32 = AR.f32(512); kk32 = AR.f32(512); g32 = AR.f32(512)
              off_alias = AR.off
              cum = AR.f32(512); epos = AR.f32(512); eneg = AR.f32(512)
              rt = AR.f32(512); tA = AR.f32(512); tB = AR.f32(512)
              atz = AR.f32(1024).rearrange("p (a b) -> p a b", b=128)
              btz = AR.f32(1024).rearrange("p (a b) -> p a b", b=128)
              ktz = AR.f32(1024).rearrange("p (a b) -> p a b", b=128)
              vz = AR.f32(1024).rearrange("p (a b) -> p a b", b=128)
              NS = 8
              AsbS = [AR.f32(512) for _ in range(NS)]
              XS = [AR.f32(128) for _ in range(NS)]
              QPS = [AR.f32(256) for _ in range(NS)]
              TsbS = [AR.f32(384) for _ in range(2)]
              Wsb = AR.f32(128); Usb = AR.f32(128)
              dtl = AR.f32(512)
              for zt, nm in ((atz, 'A:atz'), (btz, 'A:btz'), (ktz, 'A:ktz'), (vz, 'A:vz')):
                  MEMSET('pool', zt, 0.0, [nm])

              def inproj_block(wv4, wk, b, fixed=None):
                  an = fixed if fixed else next_acc(); ps = banks[an]
                  for kc in range(16):
                      MM(ps[:], wv4[:, b, kc, :], h[:, kc, :], kc == 0, kc == 15, [wk, 'h%d' % kc], [an])
                  return an, ps

              def lerp_evac(an, ps, blk, dst, dkey):
                  CPY('act', praw[:, 1:513], ps[:], [an], ['A:praw'])
                  CPY('pool', praw[:, 0:1], carry[:, blk:blk + 1], ['carry'], ['A:praw'])
                  CPY('pool', carry[:, blk:blk + 1], praw[:, 512:513], ['A:praw'], ['carry'])
                  TTo('dve', dtl, praw[:, 0:512], praw[:, 1:513], ALU.subtract, ['A:praw'], ['A:dtl'])
                  STT('dve', dst, dtl, cv('mu', blk), praw[:, 1:513], ALU.mult, ALU.add,
                      ['A:dtl', 'A:praw', 'cp'], [dkey])

              wv, wk = wload(win_d[:, 0:3].rearrange("p a k n -> p (a k n)"), 6144)
              wv4 = wv.rearrange("p (a k n) -> p a k n", a=3, k=16)
              an, ps = inproj_block(wv4, wk, 0)
              lerp_evac(an, ps, 0, tA, 'A:tA')
              ACT(txa[0:64, :], tA[0:64, :], AF.Tanh, ['A:tA'], ['A:txa'])
              ACT(txa[64:128, :], tA[64:128, :], AF.Identity, ['A:tA'], ['A:txa'])
              an, ps = inproj_block(wv4, wk, 1)
              lerp_evac(an, ps, 1, tA, 'A:tA')
              ACT(sxg0, tA, AF.Sigmoid, ['A:tA'], ['A:sxg0'])
              an, ps = inproj_block(wv4, wk, 2)
              lerp_evac(an, ps, 2, tA, 'A:tA')
              ACT(sxg1, tA, AF.Sigmoid, ['A:tA'], ['A:sxg1'])
              ck(2)

              for c in range(NPAIR):
                  cs = slice(128 * c, 128 * c + 128)
                  def chainI(cn):
                      wvn, wkn = wload(win_d[:, 3 + 3 * cn:6 + 3 * cn].rearrange("p a k n -> p (a k n)"), 6144)
                      wvn4 = wvn.rearrange("p (a k n) -> p a k n", a=3, k=16)
                      for b, (dst, dk) in enumerate(((r32, 'A:r'), (k32, 'A:k'), (v32, 'A:v'))):
                          an, ps = inproj_block(wvn4, wkn, b, fixed='acc0')
                          lerp_evac(an, ps, 3 + 3 * cn + b, dst, dk)
                          yield
                  if c == 0:
                      for _ in chainI(0):
                          pass
                  stb = banks['st']
                  rmask = cv('rmask')
                  for hh in range(2):
                      ph = slice(64 * hh, 64 * hh + 64)
                      CPY('pool', vz[ph, :, ph], v3(v32)[ph], ['A:v'], ['A:vz'])
                  MM(stb[:], w2a2[:, 0, cs], txa, True, True, ['w2a2', 'A:txa'], ['st'])
                  ACT(s32, stb[:], AF.Sigmoid, ['st', 'cp'], ['A:s'], bias=cv('w0', c))
                  P.op('dve', lambda e: e.tensor_tensor_scan(cum, rmask, s32, 0.0, ALU.mult, ALU.add),
                       ['cp', 'A:s'], ['A:cum'])
                  MM(stb[:], w2a2[:, 1, cs], txa, True, True, ['w2a2', 'A:txa'], ['st'])
                  ACT(a32, stb[:], AF.Sigmoid, ['st', 'cp'], ['A:a'], bias=cv('a0', c))
                  ACT(kk32, k32, AF.Copy, ['A:k', 'cp'], ['A:kk'], scale=cv('kk', c))
                  ACT(tA, kk32, AF.Square, ['A:kk'], ['A:tA'])
                  MM(stb[:], cv('onesblk'), tA, True, True, ['cp', 'A:tA'], ['st'])
                  ACT(tB, stb[:], AF.Sqrt, ['st'], ['A:tB'])
                  TS('dve', tA, a32, cv('ka', c), omka[:, c:c + 1], ALU.mult, ALU.add, ['A:a', 'cp', 'omka'], ['A:tA'])
                  TTo('dve', k32, k32, tA, ALU.mult, ['A:k', 'A:tA'], ['A:k'])
                  ACT(epos, cum, AF.Exp, ['A:cum'], ['A:epos'], scale=-C0)
                  ACT(eneg, cum, AF.Exp, ['A:cum'], ['A:eneg'], scale=C0)
                  TTo('pool', tA, cum, s32, ALU.subtract, ['A:cum', 'A:s'], ['A:tA'])
                  ACT(tA, tA, AF.Exp, ['A:tA'], ['A:tA'], scale=-C0)
                  TTo('dve', rt, r32, epos, ALU.mult, ['A:r', 'A:epos'], ['A:rt'])
                  TS('dve', tB, tB, 1e-12, None, ALU.max, None, ['A:tB'], ['A:tB'])
                  P.op('dve', lambda e: e.reciprocal(tB, tB), ['A:tB'], ['A:tB'])
                  TTo('dve', kk32, kk32, tB, ALU.mult, ['A:kk', 'A:tB'], ['A:kk'])
                  for hh in range(2):
                      ph = slice(64 * hh, 64 * hh + 64)
                      STT('dve', atz[ph, :, ph], v3(kk32)[ph], -1.0, v3(tA)[ph], ALU.mult, ALU.mult,
                          ['A:kk', 'A:tA'], ['A:atz'])
                      TTo('pool', ktz[ph, :, ph], v3(k32)[ph], v3(eneg)[ph], ALU.mult, ['A:k', 'A:eneg'], ['A:ktz'])
                  TTo('dve', tB, kk32, eneg, ALU.mult, ['A:kk', 'A:eneg'], ['A:tB'])
                  for hh in range(2):
                      ph = slice(64 * hh, 64 * hh + 64)
                      eng = 'dve' if hh == 0 else 'pool'
                      TTo(eng, btz[ph, :, ph], v3(tB)[ph], v3(a32)[ph], ALU.mult, ['A:tB', 'A:a'], ['A:btz'])
                  MM(stb[:], g2s[:, 0, cs], sxg0, True, False, ['g2s', 'A:sxg0'], ['st'])
                  MM(stb[:], g2s[:, 1, cs], sxg1, False, True, ['g2s', 'A:sxg1'], ['st'])
                  CPY('act', g32, stb[:], ['st'], ['A:g'])
                  ck(3)
                  Hbd = Hst[:, c, :]
                  hk = 'Hst%d' % c
                  pA = banks['pA']; pT = banks['pT']; pW = banks['pW']; pY = banks['pY']
                  dbanks = ['pD', 'st', 'acc1']
                  ddone = [False] * 8; sdone = [False] * 8

                  def chainD(j):
                      sl = j % NS
                      Asb = AsbS[sl]; Xs_ = XS[sl]; QPs_ = QPS[sl]
                      kA = 'A:Asb%d' % sl; kX = 'A:X%d' % sl; kQ = 'A:QP%d' % sl
                      bn = dbanks[sl % 3]
                      pD = banks[bn]
                      ts_ = slice(64 * j, 64 * j + 64)
                      MM(pA[:, 0:128], btz[:, j, :], atz[:, j, :], True, True, ['A:btz', 'A:atz'], ['pA'])
                      MM(pA[:, 128:256], ktz[:, j, :], atz[:, j, :], True, True, ['A:ktz', 'A:atz'], ['pA'])
                      MM(pA[:, 256:384], atz[:, j, :], btz[:, j, :], True, True, ['A:btz', 'A:atz'], ['pA'])
                      MM(pA[:, 384:448], btz[:, j, :], rt[:, ts_], True, True, ['A:btz', 'A:rt'], ['pA'])
                      MM(pA[:, 448:512], ktz[:, j, :], rt[:, ts_], True, True, ['A:ktz', 'A:rt'], ['pA'])
                      TTo('dve', Asb, pA[:], cv('amask'), ALU.mult, ['pA', 'cp'], [kA])
                      TTo('pool', Xs_, Asb[:, 0:128], cv('ident'), ALU.add, [kA, 'cp'], [kX])
                      yield
                      Qc = Asb[:, 0:128]; Pc = Asb[:, 256:384]; qk = kA
                      for stp in range(5):
                          MM(pD[:, 0:128], Pc, Qc, True, True, [qk], [bn])
                          MM(pD[:, 128:256], Qc, Pc, True, True, [qk], [bn])
                          CPY('act', QPs_, pD[:, 0:256], [bn], [kQ])
                          yield
                          Qc = QPs_[:, 0:128]; Pc = QPs_[:, 128:256]; qk = kQ
                          MM(pD[:, 256:384], Pc, Xs_, True, True, [qk, kX], [bn])
                          TTo('dve', Xs_, Xs_, pD[:, 256:384], ALU.add, [bn, kX], [kX])
                          yield
                      ddone[j] = True
                      yield

                  tr_emitted = [False] * 9
                  def emit_tr(j):
                      if j >= 8 or tr_emitted[j]:
                          return
                      tr_emitted[j] = True
                      Tsb = TsbS[j % 2]; kT = 'A:Tsb%d' % (j % 2)
                      TR(pT[:, 0:128], vz[:, j, :], ['A:vz'], ['pT'])
                      TR(pT[:, 128:256], btz[:, j, :], ['A:btz'], ['pT'])
                      TR(pT[:, 256:384], ktz[:, j, :], ['A:ktz'], ['pT'])
                      CPY('act', Tsb, pT[:, 0:384], ['pT'], [kT])

                  def chainS():
                      for j in range(8):
                          while not ddone[j]:
                              yield
                          emit_tr(j)
                          sl = j % NS
                          Asb = AsbS[sl]; Xf = XS[sl]; Tsb = TsbS[j % 2]
                          kA = 'A:Asb%d' % sl; xk = 'A:X%d' % sl; kT = 'A:Tsb%d' % (j % 2)
                          ts_ = slice(64 * j, 64 * j + 64)
                          Vz = Tsb[:, 0:128]; bT = Tsb[:, 128:256]; kT_ = Tsb[:, 256:384]
                          MM(pW[:, 0:128], atz[:, j, :], Hbd, True, False, ['A:atz', hk], ['pW'])
                          MM(pW[:, 0:128], Asb[:, 128:256], Vz, False, True, [kA, kT], ['pW'])
                          CPY('act', Wsb, pW[:, 0:128], ['pW'], ['A:Wsb'])
                          yield
                          MM(pW[:, 128:256], Xf, Wsb, True, True, [xk, 'A:Wsb'], ['pW'])
                          CPY('dve', Usb, pW[:, 128:256], ['pW'], ['A:Usb'])
                          yield
                          MM(pY[:, ts_], Hbd, rt[:, ts_], True, False, [hk, 'A:rt'], ['pY'])
                          MM(pY[:, ts_], Usb, Asb[:, 384:448], False, False, ['A:Usb', kA], ['pY'])
                          MM(pY[:, ts_], Vz, Asb[:, 448:512], False, True, [kT, kA], ['pY'])
                          MM(pW[:, 256:384], cv('ident'), Hbd, True, False, ['cp', hk], ['pW'])
                          MM(pW[:, 256:384], bT, Usb, False, False, [kT, 'A:Usb'], ['pW'])
                          MM(pW[:, 256:384], kT_, Vz, False, True, [kT], ['pW'])
                          TS('dve', Hbd, pW[:, 256:384], epos[:, 64 * j + 63:64 * j + 64], None, ALU.mult, None,
                             ['pW', 'A:epos'], [hk])
                          sdone[j] = True
                          emit_tr(j + 1)
                          yield

                  STT('dve', tA, r32, cv('rk', c), k32, ALU.mult, ALU.mult, ['A:r', 'A:k', 'cp'], ['A:tA'])
                  MM(stb[:], cv('onesblk'), tA, True, True, ['cp', 'A:tA'], ['st'])
                  TTo('dve', tB, stb[:], v32, ALU.mult, ['st', 'A:v'], ['A:tB'])

                  chains = {}
                  nextD = 0
                  sgen = chainS(); s_alive = True
                  rnd = 0
                  i_started = (c == NPAIR - 1)
                  while s_alive or chains or nextD < 8:
                      if nextD < 8 and rnd >= 2 * nextD:
                          chains[nextD] = chainD(nextD); nextD += 1
                      if not i_started and rnd >= 6:
                          chains['I'] = chainI(c + 1); i_started = True
                      if s_alive:
                          try:
                              next(sgen)
                          except StopIteration:
                              s_alive = False
                      for jj in list(chains):
                          try:
                              next(chains[jj])
                          except StopIteration:
                              del chains[jj]
                      rnd += 1
                  ck(4)
                  CPY('act', tA, pY[:], ['pY'], ['A:tA'])
                  MM(stb[:], cv('onesblkm'), tA, True, True, ['cp', 'A:tA'], ['st'])
                  TTo('dve', cum, tA, stb[:], ALU.subtract, ['A:tA', 'st'], ['A:cum'])
                  ACT(tA, cum, AF.Square, ['A:cum'], ['A:tA'])
                  MM(stb[:], cv('onesblkm'), tA, True, True, ['cp', 'A:tA'], ['st'])
                  ACT(tA, stb[:], AF.Sqrt, ['st', 'epsc'], ['A:tA'], bias=epsc[:, 1:2])
                  P.op('dve', lambda e: e.reciprocal(tA, tA), ['A:tA'], ['A:tA'])
                  TTo('dve', cum, cum, tA, ALU.mult, ['A:cum', 'A:tA'], ['A:cum'])
                  ACT(cum, cum, AF.Identity, ['A:cum', 'cp'], ['A:cum'], bias=cv('lnb', c), scale=cv('lnw', c))
                  TTo('pool', cum, cum, tB, ALU.add, ['A:cum', 'A:tB'], ['A:cum'])
                  TTo('dve', mix[:, c, :], cum, g32, ALU.mult, ['A:cum', 'A:g'], ['mix%d' % c])
                  ck(5)

              P.fence_arena()
              AR.off = off_alias
              ubuf = AR.f32(528); sA = AR.f32(528); sB = AR.f32(528)
              zbuf = AR.bf16(1024).rearrange("p (a b) -> p a b", b=512)
              for g3 in range(3):
                  nb = 3 if g3 < 2 else 2
                  wv, wk = wload(win_d[:, 27 + 3 * g3:27 + 3 * g3 + nb].rearrange("p a k n -> p (a k n)"), nb * 2048)
                  wv4 = wv.rearrange("p (a k n) -> p a k n", a=nb, k=16)
                  for b in range(nb):
                      c = 3 * g3 + b
                      gi = c // 2; m = gi + 1; win = 2 << gi
                      an, ps = inproj_block(wv4, wk, b)
                      CPY('act', ubuf[:, 16:528], ps[:], [an], ['A:ubuf'])
                      CPY('pool', ubuf[:, 0:16], carryp[:, c, :], ['carryp'], ['A:ubuf'])
                      CPY('pool', carryp[:, c, :], ubuf[:, 512:528], ['A:ubuf'], ['carryp'])
                      src = ubuf; sk = 'A:ubuf'
                      bufs = [(sA, 'A:sA'), (sB, 'A:sB')]
                      lo = 0
                      for s_ in range(m):
                          sh_ = 1 << s_
                          dst, dk = bufs[s_ % 2]
                          nlo = lo + sh_
                          TTo('pool' if s_ % 2 else 'dve', dst[:, nlo:528], src[:, nlo:528], src[:, nlo - sh_:528 - sh_],
                              ALU.add, [sk], [dk])
                          src, sk, lo = dst, dk, nlo
                      zc = zbuf[:, c % 2, :]
                      STT('dve', zc, src[:, 16:528], 1.0 / win, ubuf[:, 16:528], ALU.mult, ALU.subtract,
                          [sk, 'A:ubuf'], ['A:z%d' % (c % 2)])
                      if it == 0:
                          o_, _ = CP_OFF['invd']
                          ivd = cp[:, o_ + 16 * gi:o_ + 16 * gi + 16]
                          TTo('dve', tB[:, 0:16], src[:, 16:32], ivd, ALU.mult, [sk, 'cp'], ['A:tB'])
                          TTo('dve', zc[:, 0:16], tB[:, 0:16], ubuf[:, 16:32], ALU.subtract,
                              ['A:tB', 'A:ubuf'], ['A:z%d' % (c % 2)])
                      if c % 2 == 1:
                          for oc in range(2):
                              an2 = next_acc(); ps2 = banks[an2]
                              for k2 in range(2):
                                  MM(ps2[:], poolw[:, gi, k2, 128 * oc:128 * oc + 128], zbuf[:, k2, :], k2 == 0, k2 == 1,
                                     ['poolw', 'A:z%d' % k2], [an2])
                              cc = 2 * gi + oc
                              TS('dve', mix[:, 8 + cc, :], ps2[:], cv('poolb', cc), cv('pools', cc), ALU.add, ALU.mult,
                                 [an2, 'cp'], ['mix%d' % (8 + cc)])

              ck(6)
              for g3 in range(6):
                  nb = 3 if g3 < 5 else 1
                  wv, wk = wload(wout_d[:, 3 * g3:3 * g3 + nb].rearrange("p a k n -> p (a k n)"), nb * 2048)
                  wv4 = wv.rearrange("p (a k n) -> p a k n", a=nb, k=16)
                  for b in range(nb):
                      blk = 3 * g3 + b
                      an = next_acc(); ps = banks[an]
                      for kc in range(16):
                          MM(ps[:], wv4[:, b, kc, :], mix[:, kc, :], kc == 0, kc == 15, [wk, 'mix%d' % kc], [an])
                      STT('dve', xs[:, blk, :], ps[:], gt1[:, blk:blk + 1], xs[:, blk, :], ALU.mult, ALU.add,
                          [an, 'modsb', 'xs%d' % blk], ['xs%d' % blk])

              ck(7)
              P.fence_arena()
              norm_to_h(gs2, sh2, 'gs2')
              P.fence_arena()
              AR.reset()
              gated = AR.bf16(NFF * 512).rearrange("p (a b) -> p a b", b=512)
              ub = [AR.f32(514) for _ in range(4)]
              accb = [AR.f32(512) for _ in range(4)]
              facc = ['acc0', 'acc1', 'pA', 'pD']
              cwo, _ = CP_OFF['convw']; cbo, _ = CP_OFF['convb']
              for j in range(NFF):
                  wv, wk = wload(wup_d[:, 2 * j:2 * j + 2].rearrange("p a k n -> p (a k n)"), 4096)
                  wv4 = wv.rearrange("p (a k n) -> p a k n", a=2, k=16)
                  bi = [(2 * j) % 4, (2 * j + 1) % 4]
                  for b in range(2):
                      blk = 2 * j + b
                      an = next_acc(facc); ps = banks[an]
                      for kc in range(16):
                          MM(ps[:], wv4[:, b, kc, :], h[:, kc, :], kc == 0, kc == 15, [wk, 'h%d' % kc], [an])
                      u = ub[bi[b]]; uk = 'A:ub%d' % bi[b]; ac = accb[bi[b]]; ak = 'A:acc%d' % bi[b]
                      w_ = lambda jj: cp[:, cwo + 3 * blk + jj:cwo + 3 * blk + jj + 1]
                      CPY('act', u[:, 2:514], ps[:], [an], [uk])
                      CPY('pool', u[:, 0:2], carryf[:, blk, :], ['carryf'], [uk])
                      CPY('pool', carryf[:, blk, :], u[:, 512:514], [uk], ['carryf'])
                      TS('pool', ac, u[:, 2:514], w_(2), cp[:, cbo + blk:cbo + blk + 1], ALU.mult, ALU.add,
                         [uk, 'cp'], [ak])
                      STT('dve', ac, u[:, 1:513], w_(1), ac, ALU.mult, ALU.add, [uk, ak, 'cp'], [ak])
                      STT('dve', ac, u[:, 0:512], w_(0), ac, ALU.mult, ALU.add, [uk, ak, 'cp'], [ak])
                  ACT(accb[bi[1]], accb[bi[1]], AF.Silu, ['A:acc%d' % bi[1]], ['A:acc%d' % bi[1]])
                  TTo('dve', gated[:, j, :], accb[bi[0]], accb[bi[1]], ALU.mult, ['A:acc%d' % bi[0], 'A:acc%d' % bi[1]],
                      ['A:gated%d' % j])
              ck(8)
              for blk in range(16):
                  wv, wk = wload(wdn_d[:, blk].rearrange("p k n -> p (k n)"), NFF * 128)
                  wv3 = wv.rearrange("p (k n) -> p k n", k=NFF)
                  an = next_acc(facc); ps = banks[an]
                  for kc in range(NFF):
                      MM(ps[:], wv3[:, kc, :], gated[:, kc, :], kc == 0, kc == NFF - 1, [wk, 'A:gated%d' % kc], [an])
                  STT('dve', xs[:, blk, :], ps[:], gt2[:, blk:blk + 1], xs[:, blk, :], ALU.mult, ALU.add,
                      [an, 'modsb', 'xs%d' % blk], ['xs%d' % blk])

              ck(9)
              P.fence_arena()
              AR.reset()
              tmp_sq = [AR.bf16(512), AR.bf16(512)]
              rstd = AR.f32(512)
              tmp = [AR.f32(512), AR.f32(512)]
              ob = [AR.f32(512), AR.f32(512)]
              rms_stats(tmp_sq, rstd)
              for c in range(16):
                  t = tmp[c % 2]; o = ob[c % 2]
                  STT('dve', o, xs[:, c, :], cv('nfg', c), rstd, ALU.mult, ALU.mult, ['xs%d' % c, 'A:rstd', 'cp'],
                      ['A:ob%d' % (c % 2)])
                  P.dma('sp', outT[:, c, t0:t0 + TT], o, 'o%d' % (c % 2), reads=['A:ob%d' % (c % 2)],
                        writes=['out_%d_%d' % (it, c)])
                  outkeys.append('out_%d_%d' % (it, c))
        except _Stop:
            dump(modsb[:], 0, 96)
            dump(gs1[:], 1, 16)
        P.final_wait('sp', outkeys)
        P.emit()
    return nc


def _col(v, n):
    return np.ascontiguousarray(np.asarray(v, np.float32).reshape(n, 128).T)


def _prep_shared(inp):
    f = lambda k: np.asarray(inp[k], np.float32)
    cpk = np.zeros((128, CP_W), np.float32)
    def put(name, arr):
        o, w = CP_OFF[name]
        assert arr.shape == (128, w), (name, arr.shape, w)
        cpk[:, o:o + w] = arr
    put('bada', _col(f('b_ada')[0], 96))
    put('n1g', _col(f('norm1_g')[0], 16)); put('n2g', _col(f('norm2_g')[0], 16)); put('nfg', _col(f('norm_f_g'), 16))
    cols = np.full((35, 128), -1, np.int64)
    cols[0] = np.arange(3072, 3200); cols[1] = np.arange(3200, 3328); cols[2, :32] = np.arange(3328, 3360)
    for c in range(8):
        cols[3 + 3 * c] = np.arange(128 * c, 128 * c + 128)
        cols[4 + 3 * c] = 1024 + np.arange(128 * c, 128 * c + 128)
        cols[5 + 3 * c] = 2048 + np.arange(128 * c, 128 * c + 128)
        cols[27 + c] = 3360 + np.arange(128 * c, 128 * c + 128)
    flat = cols.reshape(-1)
    valid = flat >= 0
    w_in = f('w_in')[0]
    W = np.zeros((2048, 35 * 128), np.float32)
    W[:, valid] = w_in[:, flat[valid]]
    win = np.ascontiguousarray(W.reshape(16, 128, 35, 128).transpose(1, 2, 0, 3))
    mu = f('mu_shift')[0]
    mul = np.zeros((27 * 128,), np.float32)
    v27 = valid[:27 * 128]
    mul[v27] = mu[flat[:27 * 128][v27]]
    put('mu', _col(mul, 27))
    for nm, key in (('w0', 'w0'), ('a0', 'a0'), ('kk', 'k_k'), ('ka', 'k_a'), ('lnw', 'ln_x_w'), ('lnb', 'ln_x_b'),
                    ('pools', 'pool_scale')):
        put(nm, _col(f(key)[0], 8))
    put('rk', _col(f('r_k')[0].reshape(-1), 8))
    put('poolb', _col(f('pool_b')[0].reshape(-1), 8))
    perm = np.concatenate([np.concatenate([np.arange(128 * j, 128 * j + 128), DFF + np.arange(128 * j, 128 * j + 128)])
                           for j in range(NFF)])
    cw = f('conv_w')[0][:, perm]
    put('convw', np.ascontiguousarray(cw.reshape(3, 88, 128).transpose(2, 1, 0).reshape(128, 264)))
    put('convb', _col(f('conv_b')[0][perm], 88))
    put('ident', np.eye(128, dtype=np.float32))
    blk = np.kron(np.eye(2, dtype=np.float32), np.ones((64, 64), np.float32))
    put('onesblk', blk); put('onesblkm', blk / 64.0)
    su = np.triu(np.ones((64, 64), np.float32), 1)
    ui = np.triu(np.ones((64, 64), np.float32), 0)
    bd = lambda m: np.kron(np.eye(2, dtype=np.float32), m)
    st2 = lambda m: np.concatenate([m, m], axis=0)
    put('amask', np.concatenate([bd(su), bd(su), bd(su.T), st2(ui), st2(ui)], axis=1))
    rm = np.ones((128, 512), np.float32); rm[:, ::64] = 0.0
    put('rmask', rm)
    ivd = np.zeros((128, 64), np.float32)
    for gi in range(4):
        ivd[:, 16 * gi:16 * gi + 16] = 1.0 / np.minimum(np.arange(1, 17), 2 << gi)
    put('invd', ivd)
    sh = {}
    sh['wada'] = np.ascontiguousarray(f('w_ada')[0].reshape(16, 128, 96, 128).transpose(1, 2, 0, 3))
    sh['win'] = win
    wa = np.zeros((128, 2, 1024), np.float32)
    wa[:64, 0] = f('w2')[0]; wa[64:, 1] = f('a2')[0]
    sh['w2a2'] = wa
    g2 = f('g2')[0]
    g2l = np.zeros((128, 2, 1024), np.float32); g2l[:, 0] = g2[:128]; g2l[:32, 1] = g2[128:160]
    sh['g2'] = g2l
    sh['poolw'] = np.ascontiguousarray(f('pool_w')[0].reshape(4, 2, 128, 256).transpose(2, 0, 1, 3))
    sh['wout'] = np.ascontiguousarray(f('w_out')[0].reshape(16, 128, 16, 128).transpose(1, 2, 0, 3))
    sh['wup'] = np.ascontiguousarray(f('w_up')[0][:, perm].reshape(16, 128, 88, 128).transpose(1, 2, 0, 3))
    sh['wdn'] = np.ascontiguousarray(f('w_down')[0].reshape(44, 128, 16, 128).transpose(1, 2, 0, 3))
    return cpk, sh


def kernel(_NT=4, _cores=8, _stage=99, **inputs):
    x = np.asarray(inputs['x'], np.float32)
    c = np.asarray(inputs['c'], np.float32)
    cpk, sh = _prep_shared(inputs)
    ntok = _NT * TT
    in_maps = []
    for b in range(_cores):
        m = dict(sh)
        cb = cpk.copy()
        o, w = CP_OFF['ccol']
        cb[:, o:o + w] = _col(c[b], 16)
        m['cpack'] = cb
        m['xT'] = np.ascontiguousarray(x[b].T)
        in_maps.append(m)
    nc = build_nc(_NT, _stage)
    res = run_bass_kernel_spmd(nc, in_maps, core_ids=list(range(_cores)))
    out = np.stack([np.ascontiguousarray(res.results[b]['outT'].T) for b in range(_cores)], axis=0)
    return out[:, :ntok] if _NT < 4 else out
```

```python
import numpy as np
import concourse.bass as bass
import concourse.mybir as mybir

F32 = mybir.dt.float32
BF16 = mybir.dt.bfloat16
AF = mybir.ActivationFunctionType
ALU = mybir.AluOpType


class Prog:
    ENGS = ['pe', 'act', 'dve', 'pool', 'sp']

    def __init__(self, nc, es):
        self.nc = nc
        self.es = es
        self.sems = {}
        self.cnt = {e: 0 for e in ['pe', 'act', 'dve', 'pool']}
        self.instrs = {e: [] for e in self.ENGS}
        self.known = {e: {} for e in self.ENGS}
        self.last_w = {}
        self.readers = {}
        self.dcount = {}
        self.fence = {}

    def fence_arena(self):
        f = {}
        for d in (self.last_w, self.readers):
            for key in [k for k in d if k.startswith('A:')]:
                v = d.pop(key)
                items = [v] if isinstance(v, tuple) else list(v.items())
                for k2, v2 in items:
                    if f.get(k2, 0) < v2:
                        f[k2] = v2
        self.fence = f

    def sem(self, key):
        if key not in self.sems:
            self.sems[key] = self.es.enter_context(self.nc.semaphore(key))
        return self.sems[key]

    def _deps(self, eng, reads, writes):
        deps = {}

        def add(tok):
            if tok is None:
                return
            k, v = tok
            if deps.get(k, 0) < v:
                deps[k] = v
        for r in reads:
            add(self.last_w.get(r))
        if any(w.startswith('A:') for w in writes):
            for k, v in self.fence.items():
                add((k, v))
        for w in writes:
            add(self.last_w.get(w))
            for k, v in self.readers.get(w, {}).items():
                add((k, v))
        waits = []
        for k, v in deps.items():
            if eng == 'pe' and k == 'c_pe':
                continue
            if self.known[eng].get(k, 0) >= v:
                continue
            self.known[eng][k] = v
            waits.append((k, v))
        return waits

    def _commit(self, tok, reads, writes):
        for r in reads:
            d = self.readers.setdefault(r, {})
            if d.get(tok[0], 0) < tok[1]:
                d[tok[0]] = tok[1]
        for w in writes:
            self.last_w[w] = tok
            self.readers[w] = {}

    def op(self, eng, fn, reads=(), writes=()):
        waits = self._deps(eng, reads, writes)
        self.cnt[eng] += 1
        tok = ('c_' + eng, self.cnt[eng])
        self.sem(tok[0])
        self.instrs[eng].append((waits, fn, (tok[0], 1)))
        self._commit(tok, reads, writes)
        return tok

    def dma(self, q, out, in_, slot, reads=(), writes=(), **kw):
        waits = self._deps(q, reads, writes)
        key = 'd_' + slot
        self.sem(key)
        self.dcount[key] = self.dcount.get(key, 0) + 16
        tok = (key, self.dcount[key])
        self.instrs[q].append((waits, lambda e: e.dma_start(out=out, in_=in_, **kw), (key, 16)))
        self._commit(tok, reads, writes)
        return tok

    def settle(self, slot, regions):
        key = 'd_' + slot
        tok = (key, self.dcount[key])
        for r in regions:
            self.last_w[r] = tok

    def final_wait(self, eng, regions):
        waits = self._deps(eng, regions, ())
        self.instrs[eng].append((waits, None, None))

    def emit(self):
        nc = self.nc
        with nc.Block() as block:
            decos = {'pe': block.tensor, 'act': block.scalar, 'dve': block.vector,
                     'pool': block.gpsimd, 'sp': block.sync}
            for ename in self.ENGS:
                def make(ename):
                    def body(e):
                        for waits, fn, inc in self.instrs[ename]:
                            for k, v in waits:
                                e.wait_ge(self.sems[k], v)
                            if fn is not None:
                                ins = fn(e)
                                ins.then_inc(self.sems[inc[0]], inc[1])
                    return body
                decos[ename](make(ename))

from contextlib import ExitStack
from concourse.bass_utils import run_bass_kernel_spmd

D = 2048; T = 2048; TT = 512; NCH = 16
DR = 1024; NPAIR = 8
DFF = 5632; NFF = 44
C0 = 0.6065306597126334
RMS_EPS = 1e-6; GN_EPS = 64e-5

CP_FIELDS = [('ccol', 16), ('bada', 96), ('n1g', 16), ('n2g', 16), ('nfg', 16), ('mu', 27),
             ('w0', 8), ('a0', 8), ('kk', 8), ('ka', 8), ('rk', 8), ('lnw', 8), ('lnb', 8),
             ('poolb', 8), ('pools', 8), ('convw', 264), ('convb', 88), ('ident', 128),
             ('onesblk', 128), ('onesblkm', 128), ('amask', 512), ('rmask', 512), ('invd', 64)]
CP_OFF = {}
_o = 0
for _n, _w in CP_FIELDS:
    CP_OFF[_n] = (_o, _w); _o += _w
CP_W = _o


class _Stop(Exception):
    pass


def build_nc(NT=4, stage=99):
    nc = bass.Bass("TRN2", target_bir_lowering=False)
    es = ExitStack()
    with es:
        P = Prog(nc, es)
        din = lambda n, s: nc.dram_tensor(n, s, F32, kind="ExternalInput").ap()
        xT = din("xT", [D, T]).rearrange("(c p) t -> p c t", p=128)
        cpack_d = din("cpack", [128, CP_W])
        wada_d = din("wada", [128, 96, 16, 128])
        win_d = din("win", [128, 35, 16, 128])
        w2a2_d = din("w2a2", [128, 2, 1024])
        g2_d = din("g2", [128, 2, 1024])
        poolw_d = din("poolw", [128, 4, 2, 256])
        wout_d = din("wout", [128, 16, 16, 128])
        wup_d = din("wup", [128, 88, 16, 128])
        wdn_d = din("wdn", [128, 16, 44, 128])
        outT = nc.dram_tensor("outT", [D, T], F32, kind="ExternalOutput").ap().rearrange("(c p) t -> p c t", p=128)

        sb = lambda n, s, d: es.enter_context(nc.sbuf_tensor(n, s, d))
        psb = lambda n: es.enter_context(nc.psum_tensor(n, [128, 512], F32))

        cp = sb("cp", [128, CP_W], F32)
        def cv(name, a=None, b=None):
            o, w = CP_OFF[name]
            if a is None:
                return cp[:, o:o + w]
            return cp[:, o + a:o + (a + 1 if b is None else b)]
        xs = sb("xs", [128, 16, 512], F32)
        h = sb("h", [128, 16, 512], BF16)
        mix = sb("mix", [128, 16, 512], BF16)
        modsb = sb("modsb", [128, 96], F32)
        gs1 = sb("gs1", [128, 16], F32); gs2 = sb("gs2", [128, 16], F32)
        omka = sb("omka", [128, 8], F32)
        cact = sb("cact", [128, 16], BF16)
        ones_bf = sb("ones_bf", [128, 128], BF16)
        epsc = sb("epsc", [128, 2], F32)
        w2a2 = sb("w2a2s", [128, 2, 1024], BF16)
        g2s = sb("g2s", [128, 2, 1024], BF16)
        poolw = sb("poolws", [128, 4, 2, 256], BF16)
        Hst = sb("Hst", [128, 8, 128], F32)
        carry = sb("carry", [128, 27], F32)
        carryp = sb("carryp", [128, 8, 16], F32)
        carryf = sb("carryf", [128, 88, 2], F32)
        NSLOT = 3
        slots = [sb("wslot%d" % i, [128, 6144], BF16) for i in range(NSLOT)]
        ARENA = 20800
        arena = sb("arena", [128, ARENA], F32)
        banks = {n: psb(n) for n in ['acc0', 'acc1', 'st', 'pA', 'pD', 'pT', 'pW', 'pY']}

        class Arena:
            def __init__(self): self.off = 0
            def reset(self): self.off = 0
            def f32(self, n):
                v = arena[:, self.off:self.off + n]; self.off += n
                assert self.off <= ARENA, self.off
                return v
            def bf16(self, n):
                assert n % 2 == 0
                v = arena[:, self.off:self.off + n // 2].bitcast(BF16); self.off += n // 2
                assert self.off <= ARENA, self.off
                return v
        AR = Arena()

        def MM(out, lhsT, rhs, start, stop, r, w):
            P.op('pe', lambda e: e.matmul(out, lhsT=lhsT, rhs=rhs, start=start, stop=stop), r, w)
        def TR(out, in_, r, w):
            idt = cv('ident')
            P.op('pe', lambda e: e.transpose(out, in_, idt), list(r) + ['cp'], w)
        def ACT(out, in_, func, r, w, bias=None, scale=None):
            kw = {}
            if bias is not None: kw['bias'] = bias
            if scale is not None: kw['scale'] = scale
            P.op('act', lambda e: e.activation(out, in_, func, **kw), r, w)
        def TTo(eng, out, in0, in1, op, r, w):
            P.op(eng, lambda e: e.tensor_tensor(out, in0, in1, op), r, w)
        def TS(eng, out, in0, s1, s2, op0, op1, r, w):
            if s2 is None:
                P.op(eng, lambda e: e.tensor_scalar(out, in0, s1, None, op0), r, w)
            else:
                P.op(eng, lambda e: e.tensor_scalar(out, in0, s1, s2, op0, op1), r, w)
        def STT(eng, out, in0, scalar, in1, op0, op1, r, w):
            P.op(eng, lambda e: e.scalar_tensor_tensor(out, in0, scalar, in1, op0, op1), r, w)
        def CPY(eng, out, in_, r, w):
            if eng == 'act':
                P.op('act', lambda e: e.copy(out, in_), r, w)
            else:
                P.op(eng, lambda e: e.tensor_copy(out, in_), r, w)
        def MEMSET(eng, ap, val, w):
            P.op(eng, lambda e: e.memset(ap, val), (), w)
        def v3(ap):
            return ap.rearrange("p (a b) -> p a b", b=64)

        plan = []
        _fl = lambda ap: ap.rearrange("p a k n -> p (a k n)")
        for g in range(32):
            plan.append((_fl(wada_d[:, 3 * g:3 * g + 3]), 6144))
        for _it in range(NT):
            plan.append((_fl(win_d[:, 0:3]), 6144))
            for c in range(8):
                plan.append((_fl(win_d[:, 3 + 3 * c:6 + 3 * c]), 6144))
            for g3 in range(3):
                nb = 3 if g3 < 2 else 2
                plan.append((_fl(win_d[:, 27 + 3 * g3:27 + 3 * g3 + nb]), nb * 2048))
            for g3 in range(6):
                nb = 3 if g3 < 5 else 1
                plan.append((_fl(wout_d[:, 3 * g3:3 * g3 + nb]), nb * 2048))
            for j in range(NFF):
                plan.append((_fl(wup_d[:, 2 * j:2 * j + 2]), 4096))
            for blk in range(16):
                plan.append((wdn_d[:, blk].rearrange("p k n -> p (k n)"), NFF * 128))
        st = {'cur': 0, 'issued': 0}
        PF = 2
        def wload(src, nelem):
            k = st['cur']; st['cur'] += 1
            while st['issued'] <= min(k + PF, len(plan) - 1):
                i = st['issued']; st['issued'] += 1
                psrc, pn = plan[i]
                key = 'ws%d' % (i % NSLOT)
                P.dma('pool', slots[i % NSLOT][:, 0:pn], psrc, key, reads=(), writes=[key])
            assert plan[k][1] == nelem, (k, plan[k][1], nelem)
            return slots[k % NSLOT][:, 0:nelem], 'ws%d' % (k % NSLOT)

        P.dma('sp', cp[:], cpack_d, 'c', writes=['cp'])
        P.dma('pool', w2a2[:], w2a2_d, 'cw', writes=['w2a2'])
        P.dma('pool', g2s[:], g2_d, 'cw', writes=['g2s'])
        P.dma('pool', poolw[:], poolw_d, 'cw', writes=['poolw'])
        P.settle('cw', ['w2a2', 'g2s', 'poolw'])
        MEMSET('pool', ones_bf[:], 1.0, ['ones_bf'])
        MEMSET('pool', epsc[:, 0:1], RMS_EPS, ['epsc'])
        MEMSET('pool', epsc[:, 1:2], GN_EPS, ['epsc'])
        MEMSET('pool', Hst[:], 0.0, ['Hst%d' % c for c in range(8)])
        MEMSET('pool', carry[:], 0.0, ['carry'])
        MEMSET('pool', carryp[:], 0.0, ['carryp'])
        MEMSET('pool', carryf[:], 0.0, ['carryf'])
        ACT(cact[:], cv('ccol'), AF.Silu, ['cp'], ['cact'])
        TS('dve', omka[:], cv('ka'), -1.0, 1.0, ALU.mult, ALU.add, ['cp'], ['omka'])
        modp = banks['st']
        for g in range(32):
            wv, wk = wload(wada_d[:, 3 * g:3 * g + 3].rearrange("p a k n -> p (a k n)"), 6144)
            wv4 = wv.rearrange("p (a k n) -> p a k n", a=3, k=16)
            for a in range(3):
                j = 3 * g + a
                for kc in range(16):
                    MM(modp[:, j:j + 1], wv4[:, a, kc, :], cact[:, kc:kc + 1], kc == 0, kc == 15,
                       [wk, 'cact'], ['st'])
        TTo('dve', modsb[:], modp[:, 0:96], cv('bada'), ALU.add, ['st', 'cp'], ['modsb'])
        sh1 = modsb[:, 0:16]; sc1 = modsb[:, 16:32]; gt1 = modsb[:, 32:48]
        sh2 = modsb[:, 48:64]; sc2 = modsb[:, 64:80]; gt2 = modsb[:, 80:96]
        STT('dve', gs1[:], sc1, 1.0, cv('n1g'), ALU.add, ALU.mult, ['modsb', 'cp'], ['gs1'])
        STT('dve', gs2[:], sc2, 1.0, cv('n2g'), ALU.add, ALU.mult, ['modsb', 'cp'], ['gs2'])

        def rms_stats(tmp_sq, rstd):
            ss = banks['st']
            for c in range(16):
                sq = tmp_sq[c % 2]
                ACT(sq, xs[:, c, :], AF.Square, ['xs%d' % c], ['A:sq%d' % (c % 2)])
                MM(ss[:], ones_bf[:], sq, c == 0, c == 15, ['ones_bf', 'A:sq%d' % (c % 2)], ['st'])
            ACT(rstd, ss[:], AF.Sqrt, ['st', 'epsc'], ['A:rstd'], bias=epsc[:, 0:1], scale=1.0 / D)
            P.op('dve', lambda e: e.reciprocal(rstd, rstd), ['A:rstd'], ['A:rstd'])

        def norm_to_h(gs, sh, gkey):
            AR.reset()
            tmp_sq = [AR.bf16(512), AR.bf16(512)]
            rstd = AR.f32(512)
            tmp = [AR.f32(512), AR.f32(512)]
            rms_stats(tmp_sq, rstd)
            for c in range(16):
                t = tmp[c % 2]
                TTo('dve' if c % 2 == 0 else 'pool', t, xs[:, c, :], rstd, ALU.mult,
                    ['xs%d' % c, 'A:rstd'], ['A:nt%d' % (c % 2)])
                ACT(h[:, c, :], t, AF.Identity, ['A:nt%d' % (c % 2), gkey, 'modsb'], ['h%d' % c],
                    bias=sh[:, c:c + 1], scale=gs[:, c:c + 1])

        hkeys = ['h%d' % c for c in range(16)]
        mixkeys = ['mix%d' % c for c in range(16)]
        accs = ['acc0', 'acc1']
        accn = {'n': 0}
        def next_acc(names=accs):
            n = names[accn['n'] % len(names)]; accn['n'] += 1
            return n

        outkeys = []
        def ck(n):
            if stage == n:
                raise _Stop()
        def dump(ap, slot, w=512):
            P.dma('sp', outT[:, slot, 0:w], ap, 'dbg', reads=['cp', 'modsb', 'gs1'] + ['A:tA', 'A:d', 'A:kk', 'A:k', 'A:r', 'A:v', 'A:s', 'A:a', 'A:g'] * (slot >= 4), writes=['dbg%d' % slot])
            outkeys.append('dbg%d' % slot)
        try:
          ck(0)
          for it in range(NT):
              t0 = it * TT
              for q in range(4):
                  P.dma('sp', xs[:, 4 * q:4 * q + 4, :], xT[:, 4 * q:4 * q + 4, t0:t0 + TT], 'x',
                        writes=['xs%d' % c for c in range(4 * q, 4 * q + 4)])
              P.settle('x', ['xs%d' % c for c in range(16)])
              P.fence_arena()
              norm_to_h(gs1, sh1, 'gs1')
              ck(1)
              P.fence_arena()

              AR.reset()
              praw = AR.f32(528)
              txa = AR.bf16(512); sxg0 = AR.bf16(512); sxg1 = AR.bf16(512)
              r32 = AR.f32(512); k32 = AR.f32(512); v32 = AR.f32(512); a32 = AR.f32(512)
              s32 = AR.f32(512); kk32 = AR.f32(512); g32 = AR.f32(512)
              off_alias = AR.off
              cum = AR.f32(512); epos = AR.f32(512); eneg = AR.f32(512)
              rt = AR.f32(512); tA = AR.f32(512); tB = AR.f32(512)
              atz = AR.f32(1024).rearrange("p (a b) -> p a b", b=128)
              btz = AR.f32(1024).rearrange("p (a b) -> p a b", b=128)
              ktz = AR.f32(1024).rearrange("p (a b) -> p a b", b=128)
              vz = AR.f32(1024).rearrange("p (a b) -> p a b", b=128)
              NS = 8
              AsbS = [AR.f32(512) for _ in range(NS)]
              XS = [AR.f32(128) for _ in range(NS)]
              QPS = [AR.f32(256) for _ in range(NS)]
              TsbS = [AR.f32(384) for _ in range(2)]
              Wsb = AR.f32(128); Usb = AR.f32(128)
              dtl = AR.f32(512)
              for zt, nm in ((atz, 'A:atz'), (btz, 'A:btz'), (ktz, 'A:ktz'), (vz, 'A:vz')):
                  MEMSET('pool', zt, 0.0, [nm])

              def inproj_block(wv4, wk, b, fixed=None):
                  an = fixed if fixed else next_acc(); ps = banks[an]
                  for kc in range(16):
                      MM(ps[:], wv4[:, b, kc, :], h[:, kc, :], kc == 0, kc == 15, [wk, 'h%d' % kc], [an])
                  return an, ps

              def lerp_evac(an, ps, blk, dst, dkey):
                  CPY('act', praw[:, 1:513], ps[:], [an], ['A:praw'])
                  CPY('pool', praw[:, 0:1], carry[:, blk:blk + 1], ['carry'], ['A:praw'])
                  CPY('pool', carry[:, blk:blk + 1], praw[:, 512:513], ['A:praw'], ['carry'])
                  TTo('dve', dtl, praw[:, 0:512], praw[:, 1:513], ALU.subtract, ['A:praw'], ['A:dtl'])
                  STT('dve', dst, dtl, cv('mu', blk), praw[:, 1:513], ALU.mult, ALU.add,
                      ['A:dtl', 'A:praw', 'cp'], [dkey])

              wv, wk = wload(win_d[:, 0:3].rearrange("p a k n -> p (a k n)"), 6144)
              wv4 = wv.rearrange("p (a k n) -> p a k n", a=3, k=16)
              an, ps = inproj_block(wv4, wk, 0)
              lerp_evac(an, ps, 0, tA, 'A:tA')
              ACT(txa[0:64, :], tA[0:64, :], AF.Tanh, ['A:tA'], ['A:txa'])
              ACT(txa[64:128, :], tA[64:128, :], AF.Identity, ['A:tA'], ['A:txa'])
              an, ps = inproj_block(wv4, wk, 1)
              lerp_evac(an, ps, 1, tA, 'A:tA')
              ACT(sxg0, tA, AF.Sigmoid, ['A:tA'], ['A:sxg0'])
              an, ps = inproj_block(wv4, wk, 2)
              lerp_evac(an, ps, 2, tA, 'A:tA')
              ACT(sxg1, tA, AF.Sigmoid, ['A:tA'], ['A:sxg1'])
              ck(2)

              for c in range(NPAIR):
                  cs = slice(128 * c, 128 * c + 128)
                  def chainI(cn, fixed='acc0'):
                      wvn, wkn = wload(win_d[:, 3 + 3 * cn:6 + 3 * cn].rearrange("p a k n -> p (a k n)"), 6144)
                      wvn4 = wvn.rearrange("p (a k n) -> p a k n", a=3, k=16)
                      for b, (dst, dk) in enumerate(((r32, 'A:r'), (k32, 'A:k'), (v32, 'A:v'))):
                          an, ps = inproj_block(wvn4, wkn, b, fixed=fixed)
                          lerp_evac(an, ps, 3 + 3 * cn + b, dst, dk)
                          yield
                  if True:
                      for _ in chainI(c, None):
                          pass
                  stb = banks['st']
                  rmask = cv('rmask')
                  for hh in range(2):
                      ph = slice(64 * hh, 64 * hh + 64)
                      CPY('pool', vz[ph, :, ph], v3(v32)[ph], ['A:v'], ['A:vz'])
                  MM(stb[:], w2a2[:, 0, cs], txa, True, True, ['w2a2', 'A:txa'], ['st'])
                  ACT(s32, stb[:], AF.Sigmoid, ['st', 'cp'], ['A:s'], bias=cv('w0', c))
                  P.op('dve', lambda e: e.tensor_tensor_scan(cum, rmask, s32, 0.0, ALU.mult, ALU.add),
                       ['cp', 'A:s'], ['A:cum'])
                  MM(stb[:], w2a2[:, 1, cs], txa, True, True, ['w2a2', 'A:txa'], ['st'])
                  ACT(a32, stb[:], AF.Sigmoid, ['st', 'cp'], ['A:a'], bias=cv('a0', c))
                  ACT(kk32, k32, AF.Copy, ['A:k', 'cp'], ['A:kk'], scale=cv('kk', c))
                  ACT(tA, kk32, AF.Square, ['A:kk'], ['A:tA'])
                  MM(stb[:], cv('onesblk'), tA, True, True, ['cp', 'A:tA'], ['st'])
                  ACT(tB, stb[:], AF.Sqrt, ['st'], ['A:tB'])
                  TS('dve', tA, a32, cv('ka', c), omka[:, c:c + 1], ALU.mult, ALU.add, ['A:a', 'cp', 'omka'], ['A:tA'])
                  TTo('dve', k32, k32, tA, ALU.mult, ['A:k', 'A:tA'], ['A:k'])
                  ACT(epos, cum, AF.Exp, ['A:cum'], ['A:epos'], scale=-C0)
                  ACT(eneg, cum, AF.Exp, ['A:cum'], ['A:eneg'], scale=C0)
                  TTo('pool', tA, cum, s32, ALU.subtract, ['A:cum', 'A:s'], ['A:tA'])
                  ACT(tA, tA, AF.Exp, ['A:tA'], ['A:tA'], scale=-C0)
                  TTo('dve', rt, r32, epos, ALU.mult, ['A:r', 'A:epos'], ['A:rt'])
                  TS('dve', tB, tB, 1e-12, None, ALU.max, None, ['A:tB'], ['A:tB'])
                  P.op('dve', lambda e: e.reciprocal(tB, tB), ['A:tB'], ['A:tB'])
                  TTo('dve', kk32, kk32, tB, ALU.mult, ['A:kk', 'A:tB'], ['A:kk'])
                  for hh in range(2):
                      ph = slice(64 * hh, 64 * hh + 64)
                      STT('dve', atz[ph, :, ph], v3(kk32)[ph], -1.0, v3(tA)[ph], ALU.mult, ALU.mult,
                          ['A:kk', 'A:tA'], ['A:atz'])
                      TTo('pool', ktz[ph, :, ph], v3(k32)[ph], v3(eneg)[ph], ALU.mult, ['A:k', 'A:eneg'], ['A:ktz'])
                  TTo('dve', tB, kk32, eneg, ALU.mult, ['A:kk', 'A:eneg'], ['A:tB'])
                  for hh in range(2):
                      ph = slice(64 * hh, 64 * hh + 64)
                      eng = 'dve' if hh == 0 else 'pool'
                      TTo(eng, btz[ph, :, ph], v3(tB)[ph], v3(a32)[ph], ALU.mult, ['A:tB', 'A:a'], ['A:btz'])
                  MM(stb[:], g2s[:, 0, cs], sxg0, True, False, ['g2s', 'A:sxg0'], ['st'])
                  MM(stb[:], g2s[:, 1, cs], sxg1, False, True, ['g2s', 'A:sxg1'], ['st'])
                  CPY('act', g32, stb[:], ['st'], ['A:g'])
                  ck(3)
                  Hbd = Hst[:, c, :]
                  hk = 'Hst%d' % c
                  pA = banks['pA']; pT = banks['pT']; pW = banks['pW']; pY = banks['pY']
                  _STAG = 3; _HOIST = 0; _NB = 4
                  dbanks = ['pD', 'st', 'acc1', 'acc0'][:_NB]
                  ddone = [False] * 8; sdone = [False] * 8

                  def chainD(j):
                      sl = j % NS
                      Asb = AsbS[sl]; Xs_ = XS[sl]; QPs_ = QPS[sl]
                      kA = 'A:Asb%d' % sl; kX = 'A:X%d' % sl; kQ = 'A:QP%d' % sl
                      bn = dbanks[sl % len(dbanks)]
                      pD = banks[bn]
                      ts_ = slice(64 * j, 64 * j + 64)
                      MM(pA[:, 0:128], btz[:, j, :], atz[:, j, :], True, True, ['A:btz', 'A:atz'], ['pA'])
                      MM(pA[:, 128:256], ktz[:, j, :], atz[:, j, :], True, True, ['A:ktz', 'A:atz'], ['pA'])
                      MM(pA[:, 256:384], atz[:, j, :], btz[:, j, :], True, True, ['A:btz', 'A:atz'], ['pA'])
                      MM(pA[:, 384:448], btz[:, j, :], rt[:, ts_], True, True, ['A:btz', 'A:rt'], ['pA'])
                      MM(pA[:, 448:512], ktz[:, j, :], rt[:, ts_], True, True, ['A:ktz', 'A:rt'], ['pA'])
                      TTo('dve', Asb, pA[:], cv('amask'), ALU.mult, ['pA', 'cp'], [kA])
                      TTo('pool', Xs_, Asb[:, 0:128], cv('ident'), ALU.add, [kA, 'cp'], [kX])
                      yield
                      Qc = Asb[:, 0:128]; Pc = Asb[:, 256:384]; qk = kA
                      for stp in range(5):
                          MM(pD[:, 0:128], Pc, Qc, True, True, [qk], [bn])
                          MM(pD[:, 128:256], Qc, Pc, True, True, [qk], [bn])
                          CPY('act', QPs_, pD[:, 0:256], [bn], [kQ])
                          yield
                          Qc = QPs_[:, 0:128]; Pc = QPs_[:, 128:256]; qk = kQ
                          MM(pD[:, 256:384], Pc, Xs_, True, True, [qk, kX], [bn])
                          TTo('dve', Xs_, Xs_, pD[:, 256:384], ALU.add, [bn, kX], [kX])
                          yield
                      ddone[j] = True
                      yield

                  tr_emitted = [False] * 9
                  def emit_tr(j):
                      if j >= 8 or tr_emitted[j]:
                          return
                      tr_emitted[j] = True
                      Tsb = TsbS[j % 2]; kT = 'A:Tsb%d' % (j % 2)
                      TR(pT[:, 0:128], vz[:, j, :], ['A:vz'], ['pT'])
                      TR(pT[:, 128:256], btz[:, j, :], ['A:btz'], ['pT'])
                      TR(pT[:, 256:384], ktz[:, j, :], ['A:ktz'], ['pT'])
                      CPY('act', Tsb, pT[:, 0:384], ['pT'], [kT])

                  def chainS():
                      for j in range(8):
                          while not ddone[j]:
                              yield
                          emit_tr(j)
                          sl = j % NS
                          Asb = AsbS[sl]; Xf = XS[sl]; Tsb = TsbS[j % 2]
                          kA = 'A:Asb%d' % sl; xk = 'A:X%d' % sl; kT = 'A:Tsb%d' % (j % 2)
                          ts_ = slice(64 * j, 64 * j + 64)
                          Vz = Tsb[:, 0:128]; bT = Tsb[:, 128:256]; kT_ = Tsb[:, 256:384]
                          MM(pW[:, 0:128], atz[:, j, :], Hbd, True, False, ['A:atz', hk], ['pW'])
                          MM(pW[:, 0:128], Asb[:, 128:256], Vz, False, True, [kA, kT], ['pW'])
                          CPY('act', Wsb, pW[:, 0:128], ['pW'], ['A:Wsb'])
                          yield
                          MM(pW[:, 128:256], Xf, Wsb, True, True, [xk, 'A:Wsb'], ['pW'])
                          CPY('dve', Usb, pW[:, 128:256], ['pW'], ['A:Usb'])
                          yield
                          MM(pY[:, ts_], Hbd, rt[:, ts_], True, False, [hk, 'A:rt'], ['pY'])
                          MM(pY[:, ts_], Usb, Asb[:, 384:448], False, False, ['A:Usb', kA], ['pY'])
                          MM(pY[:, ts_], Vz, Asb[:, 448:512], False, True, [kT, kA], ['pY'])
                          MM(pW[:, 256:384], cv('ident'), Hbd, True, False, ['cp', hk], ['pW'])
                          MM(pW[:, 256:384], bT, Usb, False, False, [kT, 'A:Usb'], ['pW'])
                          MM(pW[:, 256:384], kT_, Vz, False, True, [kT], ['pW'])
                          TS('dve', Hbd, pW[:, 256:384], epos[:, 64 * j + 63:64 * j + 64], None, ALU.mult, None,
                             ['pW', 'A:epos'], [hk])
                          sdone[j] = True
                          emit_tr(j + 1)
                          yield

                  STT('dve', tA, r32, cv('rk', c), k32, ALU.mult, ALU.mult, ['A:r', 'A:k', 'cp'], ['A:tA'])
                  MM(stb[:], cv('onesblk'), tA, True, True, ['cp', 'A:tA'], ['st'])
                  TTo('dve', tB, stb[:], v32, ALU.mult, ['st', 'A:v'], ['A:tB'])

                  chains = {}
                  nextD = 0
                  sgen = chainS(); s_alive = True
                  rnd = 0
                  i_started = (c == NPAIR - 1)
                  while s_alive or chains or nextD < 8:
                      if nextD < 8 and rnd >= _STAG * nextD:
                          chains[nextD] = chainD(nextD); nextD += 1
                      if _HOIST and not i_started and rnd >= _HOIST:
                          chains['I'] = chainI(c + 1); i_started = True
                      if s_alive:
                          try:
                              next(sgen)
                          except StopIteration:
                              s_alive = False
                      for jj in list(chains):
                          try:
                              next(chains[jj])
                          except StopIteration:
                              del chains[jj]
                      rnd += 1
                  ck(4)
                  CPY('act', tA, pY[:], ['pY'], ['A:tA'])
                  MM(stb[:], cv('onesblkm'), tA, True, True, ['cp', 'A:tA'], ['st'])
                  TTo('dve', cum, tA, stb[:], ALU.subtract, ['A:tA', 'st'], ['A:cum'])
                  ACT(tA, cum, AF.Square, ['A:cum'], ['A:tA'])
                  MM(stb[:], cv('onesblkm'), tA, True, True, ['cp', 'A:tA'], ['st'])
                  ACT(tA, stb[:], AF.Sqrt, ['st', 'epsc'], ['A:tA'], bias=epsc[:, 1:2])
                  P.op('dve', lambda e: e.reciprocal(tA, tA), ['A:tA'], ['A:tA'])
                  TTo('dve', cum, cum, tA, ALU.mult, ['A:cum', 'A:tA'], ['A:cum'])
                  ACT(cum, cum, AF.Identity, ['A:cum', 'cp'], ['A:cum'], bias=cv('lnb', c), scale=cv('lnw', c))
                  TTo('pool', cum, cum, tB, ALU.add, ['A:cum', 'A:tB'], ['A:cum'])
                  TTo('dve', mix[:, c, :], cum, g32, ALU.mult, ['A:cum', 'A:g'], ['mix%d' % c])
                  ck(5)

              P.fence_arena()
              AR.off = off_alias
              ubuf = AR.f32(528); sA = AR.f32(528); sB = AR.f32(528)
              zbuf = AR.bf16(1024).rearrange("p (a b) -> p a b", b=512)
              for g3 in range(3):
                  nb = 3 if g3 < 2 else 2
                  wv, wk = wload(win_d[:, 27 + 3 * g3:27 + 3 * g3 + nb].rearrange("p a k n -> p (a k n)"), nb * 2048)
                  wv4 = wv.rearrange("p (a k n) -> p a k n", a=nb, k=16)
                  for b in range(nb):
                      c = 3 * g3 + b
                      gi = c // 2; m = gi + 1; win = 2 << gi
                      an, ps = inproj_block(wv4, wk, b)
                      CPY('act', ubuf[:, 16:528], ps[:], [an], ['A:ubuf'])
                      CPY('pool', ubuf[:, 0:16], carryp[:, c, :], ['carryp'], ['A:ubuf'])
                      CPY('pool', carryp[:, c, :], ubuf[:, 512:528], ['A:ubuf'], ['carryp'])
                      src = ubuf; sk = 'A:ubuf'
                      bufs = [(sA, 'A:sA'), (sB, 'A:sB')]
                      lo = 0
                      for s_ in range(m):
                          sh_ = 1 << s_
                          dst, dk = bufs[s_ % 2]
                          nlo = lo + sh_
                          TTo('pool' if s_ % 2 else 'dve', dst[:, nlo:528], src[:, nlo:528], src[:, nlo - sh_:528 - sh_],
                              ALU.add, [sk], [dk])
                          src, sk, lo = dst, dk, nlo
                      zc = zbuf[:, c % 2, :]
                      STT('dve', zc, src[:, 16:528], 1.0 / win, ubuf[:, 16:528], ALU.mult, ALU.subtract,
                          [sk, 'A:ubuf'], ['A:z%d' % (c % 2)])
                      if it == 0:
                          o_, _ = CP_OFF['invd']
                          ivd = cp[:, o_ + 16 * gi:o_ + 16 * gi + 16]
                          TTo('dve', tB[:, 0:16], src[:, 16:32], ivd, ALU.mult, [sk, 'cp'], ['A:tB'])
                          TTo('dve', zc[:, 0:16], tB[:, 0:16], ubuf[:, 16:32], ALU.subtract,
                              ['A:tB', 'A:ubuf'], ['A:z%d' % (c % 2)])
                      if c % 2 == 1:
                          for oc in range(2):
                              an2 = next_acc(); ps2 = banks[an2]
                              for k2 in range(2):
                                  MM(ps2[:], poolw[:, gi, k2, 128 * oc:128 * oc + 128], zbuf[:, k2, :], k2 == 0, k2 == 1,
                                     ['poolw', 'A:z%d' % k2], [an2])
                              cc = 2 * gi + oc
                              TS('dve', mix[:, 8 + cc, :], ps2[:], cv('poolb', cc), cv('pools', cc), ALU.add, ALU.mult,
                                 [an2, 'cp'], ['mix%d' % (8 + cc)])

              ck(6)
              for g3 in range(6):
                  nb = 3 if g3 < 5 else 1
                  wv, wk = wload(wout_d[:, 3 * g3:3 * g3 + nb].rearrange("p a k n -> p (a k n)"), nb * 2048)
                  wv4 = wv.rearrange("p (a k n) -> p a k n", a=nb, k=16)
                  for b in range(nb):
                      blk = 3 * g3 + b
                      an = next_acc(); ps = banks[an]
                      for kc in range(16):
                          MM(ps[:], wv4[:, b, kc, :], mix[:, kc, :], kc == 0, kc == 15, [wk, 'mix%d' % kc], [an])
                      STT('dve', xs[:, blk, :], ps[:], gt1[:, blk:blk + 1], xs[:, blk, :], ALU.mult, ALU.add,
                          [an, 'modsb', 'xs%d' % blk], ['xs%d' % blk])

              ck(7)
              P.fence_arena()
              norm_to_h(gs2, sh2, 'gs2')
              P.fence_arena()
              AR.reset()
              gated = AR.bf16(NFF * 512).rearrange("p (a b) -> p a b", b=512)
              ub = [AR.f32(514) for _ in range(4)]
              accb = [AR.f32(512) for _ in range(4)]
              facc = ['acc0', 'acc1', 'pA', 'pD']
              cwo, _ = CP_OFF['convw']; cbo, _ = CP_OFF['convb']
              for j in range(NFF):
                  wv, wk = wload(wup_d[:, 2 * j:2 * j + 2].rearrange("p a k n -> p (a k n)"), 4096)
                  wv4 = wv.rearrange("p (a k n) -> p a k n", a=2, k=16)
                  bi = [(2 * j) % 4, (2 * j + 1) % 4]
                  for b in range(2):
                      blk = 2 * j + b
                      an = next_acc(facc); ps = banks[an]
                      for kc in range(16):
                          MM(ps[:], wv4[:, b, kc, :], h[:, kc, :], kc == 0, kc == 15, [wk, 'h%d' % kc], [an])
                      u = ub[bi[b]]; uk = 'A:ub%d' % bi[b]; ac = accb[bi[b]]; ak = 'A:acc%d' % bi[b]
                      w_ = lambda jj: cp[:, cwo + 3 * blk + jj:cwo + 3 * blk + jj + 1]
                      CPY('act', u[:, 2:514], ps[:], [an], [uk])
                      CPY('pool', u[:, 0:2], carryf[:, blk, :], ['carryf'], [uk])
                      CPY('pool', carryf[:, blk, :], u[:, 512:514], [uk], ['carryf'])
                      TS('pool', ac, u[:, 2:514], w_(2), cp[:, cbo + blk:cbo + blk + 1], ALU.mult, ALU.add,
                         [uk, 'cp'], [ak])
                      STT('dve', ac, u[:, 1:513], w_(1), ac, ALU.mult, ALU.add, [uk, ak, 'cp'], [ak])
                      STT('dve', ac, u[:, 0:512], w_(0), ac, ALU.mult, ALU.add, [uk, ak, 'cp'], [ak])
                  ACT(accb[bi[1]], accb[bi[1]], AF.Silu, ['A:acc%d' % bi[1]], ['A:acc%d' % bi[1]])
                  TTo('dve', gated[:, j, :], accb[bi[0]], accb[bi[1]], ALU.mult, ['A:acc%d' % bi[0], 'A:acc%d' % bi[1]],
                      ['A:gated%d' % j])
              ck(8)
              for blk in range(16):
                  wv, wk = wload(wdn_d[:, blk].rearrange("p k n -> p (k n)"), NFF * 128)
                  wv3 = wv.rearrange("p (k n) -> p k n", k=NFF)
                  an = next_acc(facc); ps = banks[an]
                  for kc in range(NFF):
                      MM(ps[:], wv3[:, kc, :], gated[:, kc, :], kc == 0, kc == NFF - 1, [wk, 'A:gated%d' % kc], [an])
                  STT('dve', xs[:, blk, :], ps[:], gt2[:, blk:blk + 1], xs[:, blk, :], ALU.mult, ALU.add,
                      [an, 'modsb', 'xs%d' % blk], ['xs%d' % blk])

              ck(9)
              P.fence_arena()
              AR.reset()
              tmp_sq = [AR.bf16(512), AR.bf16(512)]
              rstd = AR.f32(512)
              tmp = [AR.f32(512), AR.f32(512)]
              ob = [AR.f32(512), AR.f32(512)]
              rms_stats(tmp_sq, rstd)
              for c in range(16):
                  t = tmp[c % 2]; o = ob[c % 2]
                  STT('dve', o, xs[:, c, :], cv('nfg', c), rstd, ALU.mult, ALU.mult, ['xs%d' % c, 'A:rstd', 'cp'],
                      ['A:ob%d' % (c % 2)])
                  P.dma('sp', outT[:, c, t0:t0 + TT], o, 'o%d' % (c % 2), reads=['A:ob%d' % (c % 2)],
                        writes=['out_%d_%d' % (it, c)])
                  outkeys.append('out_%d_%d' % (it, c))
        except _Stop:
            dump(modsb[:], 0, 96)
            dump(gs1[:], 1, 16)
        P.final_wait('sp', outkeys)
        P.emit()
    return nc


def _col(v, n):
    return np.ascontiguousarray(np.asarray(v, np.float32).reshape(n, 128).T)


def _prep_shared(inp):
    f = lambda k: np.asarray(inp[k], np.float32)
    cpk = np.zeros((128, CP_W), np.float32)
    def put(name, arr):
        o, w = CP_OFF[name]
        assert arr.shape == (128, w), (name, arr.shape, w)
        cpk[:, o:o + w] = arr
    put('bada', _col(f('b_ada')[0], 96))
    put('n1g', _col(f('norm1_g')[0], 16)); put('n2g', _col(f('norm2_g')[0], 16)); put('nfg', _col(f('norm_f_g'), 16))
    cols = np.full((35, 128), -1, np.int64)
    cols[0] = np.arange(3072, 3200); cols[1] = np.arange(3200, 3328); cols[2, :32] = np.arange(3328, 3360)
    for c in range(8):
        cols[3 + 3 * c] = np.arange(128 * c, 128 * c + 128)
        cols[4 + 3 * c] = 1024 + np.arange(128 * c, 128 * c + 128)
        cols[5 + 3 * c] = 2048 + np.arange(128 * c, 128 * c + 128)
        cols[27 + c] = 3360 + np.arange(128 * c, 128 * c + 128)
    flat = cols.reshape(-1)
    valid = flat >= 0
    w_in = f('w_in')[0]
    W = np.zeros((2048, 35 * 128), np.float32)
    W[:, valid] = w_in[:, flat[valid]]
    win = np.ascontiguousarray(W.reshape(16, 128, 35, 128).transpose(1, 2, 0, 3))
    mu = f('mu_shift')[0]
    mul = np.zeros((27 * 128,), np.float32)
    v27 = valid[:27 * 128]
    mul[v27] = mu[flat[:27 * 128][v27]]
    put('mu', _col(mul, 27))
    for nm, key in (('w0', 'w0'), ('a0', 'a0'), ('kk', 'k_k'), ('ka', 'k_a'), ('lnw', 'ln_x_w'), ('lnb', 'ln_x_b'),
                    ('pools', 'pool_scale')):
        put(nm, _col(f(key)[0], 8))
    put('rk', _col(f('r_k')[0].reshape(-1), 8))
    put('poolb', _col(f('pool_b')[0].reshape(-1), 8))
    perm = np.concatenate([np.concatenate([np.arange(128 * j, 128 * j + 128), DFF + np.arange(128 * j, 128 * j + 128)])
                           for j in range(NFF)])
    cw = f('conv_w')[0][:, perm]
    put('convw', np.ascontiguousarray(cw.reshape(3, 88, 128).transpose(2, 1, 0).reshape(128, 264)))
    put('convb', _col(f('conv_b')[0][perm], 88))
    put('ident', np.eye(128, dtype=np.float32))
    blk = np.kron(np.eye(2, dtype=np.float32), np.ones((64, 64), np.float32))
    put('onesblk', blk); put('onesblkm', blk / 64.0)
    su = np.triu(np.ones((64, 64), np.float32), 1)
    ui = np.triu(np.ones((64, 64), np.float32), 0)
    bd = lambda m: np.kron(np.eye(2, dtype=np.float32), m)
    st2 = lambda m: np.concatenate([m, m], axis=0)
    put('amask', np.concatenate([bd(su), bd(su), bd(su.T), st2(ui), st2(ui)], axis=1))
    rm = np.ones((128, 512), np.float32); rm[:, ::64] = 0.0
    put('rmask', rm)
    ivd = np.zeros((128, 64), np.float32)
    for gi in range(4):
        ivd[:, 16 * gi:16 * gi + 16] = 1.0 / np.minimum(np.arange(1, 17), 2 << gi)
    put('invd', ivd)
    sh = {}
    sh['wada'] = np.ascontiguousarray(f('w_ada')[0].reshape(16, 128, 96, 128).transpose(1, 2, 0, 3))
    sh['win'] = win
    wa = np.zeros((128, 2, 1024), np.float32)
    wa[:64, 0] = f('w2')[0]; wa[64:, 1] = f('a2')[0]
    sh['w2a2'] = wa
    g2 = f('g2')[0]
    g2l = np.zeros((128, 2, 1024), np.float32); g2l[:, 0] = g2[:128]; g2l[:32, 1] = g2[128:160]
    sh['g2'] = g2l
    sh['poolw'] = np.ascontiguousarray(f('pool_w')[0].reshape(4, 2, 128, 256).transpose(2, 0, 1, 3))
    sh['wout'] = np.ascontiguousarray(f('w_out')[0].reshape(16, 128, 16, 128).transpose(1, 2, 0, 3))
    sh['wup'] = np.ascontiguousarray(f('w_up')[0][:, perm].reshape(16, 128, 88, 128).transpose(1, 2, 0, 3))
    sh['wdn'] = np.ascontiguousarray(f('w_down')[0].reshape(44, 128, 16, 128).transpose(1, 2, 0, 3))
    return cpk, sh


def kernel(_NT=4, _cores=8, _stage=99, **inputs):
    x = np.asarray(inputs['x'], np.float32)
    c = np.asarray(inputs['c'], np.float32)
    cpk, sh = _prep_shared(inputs)
    ntok = _NT * TT
    in_maps = []
    for b in range(_cores):
        m = dict(sh)
        cb = cpk.copy()
        o, w = CP_OFF['ccol']
        cb[:, o:o + w] = _col(c[b], 16)
        m['cpack'] = cb
        m['xT'] = np.ascontiguousarray(x[b].T)
        in_maps.append(m)
    nc = build_nc(_NT, _stage)
    res = run_bass_kernel_spmd(nc, in_maps, core_ids=list(range(_cores)))
    out = np.stack([np.ascontiguousarray(res.results[b]['outT'].T) for b in range(_cores)], axis=0)
    return out[:, :ntok] if _NT < 4 else out
```

```python
import numpy as np
import concourse.bass as bass
import concourse.mybir as mybir

F32 = mybir.dt.float32
BF16 = mybir.dt.bfloat16
AF = mybir.ActivationFunctionType
ALU = mybir.AluOpType


class Prog:
    ENGS = ['pe', 'act', 'dve', 'pool', 'sp']

    def __init__(self, nc, es):
        self.nc = nc
        self.es = es
        self.sems = {}
        self.cnt = {e: 0 for e in ['pe', 'act', 'dve', 'pool']}
        self.instrs = {e: [] for e in self.ENGS}
        self.known = {e: {} for e in self.ENGS}
        self.last_w = {}
        self.readers = {}
        self.dcount = {}
        self.fence = {}

    def fence_arena(self):
        f = {}
        for d in (self.last_w, self.readers):
            for key in [k for k in d if k.startswith('A:')]:
                v = d.pop(key)
                items = [v] if isinstance(v, tuple) else list(v.items())
                for k2, v2 in items:
                    if f.get(k2, 0) < v2:
                        f[k2] = v2
        self.fence = f

    def sem(self, key):
        if key not in self.sems:
            self.sems[key] = self.es.enter_context(self.nc.semaphore(key))
        return self.sems[key]

    def _deps(self, eng, reads, writes):
        deps = {}

        def add(tok):
            if tok is None:
                return
            k, v = tok
            if deps.get(k, 0) < v:
                deps[k] = v
        for r in reads:
            add(self.last_w.get(r))
        if any(w.startswith('A:') for w in writes):
            for k, v in self.fence.items():
                add((k, v))
        for w in writes:
            add(self.last_w.get(w))
            for k, v in self.readers.get(w, {}).items():
                add((k, v))
        waits = []
        for k, v in deps.items():
            if eng == 'pe' and k == 'c_pe':
                continue
            if self.known[eng].get(k, 0) >= v:
                continue
            self.known[eng][k] = v
            waits.append((k, v))
        return waits

    def _commit(self, tok, reads, writes):
        for r in reads:
            d = self.readers.setdefault(r, {})
            if d.get(tok[0], 0) < tok[1]:
                d[tok[0]] = tok[1]
        for w in writes:
            self.last_w[w] = tok
            self.readers[w] = {}

    def op(self, eng, fn, reads=(), writes=()):
        waits = self._deps(eng, reads, writes)
        self.cnt[eng] += 1
        tok = ('c_' + eng, self.cnt[eng])
        self.sem(tok[0])
        self.instrs[eng].append((waits, fn, (tok[0], 1)))
        self._commit(tok, reads, writes)
        return tok

    def dma(self, q, out, in_, slot, reads=(), writes=(), **kw):
        waits = self._deps(q, reads, writes)
        key = 'd_' + slot
        self.sem(key)
        self.dcount[key] = self.dcount.get(key, 0) + 16
        tok = (key, self.dcount[key])
        self.instrs[q].append((waits, lambda e: e.dma_start(out=out, in_=in_, **kw), (key, 16)))
        self._commit(tok, reads, writes)
        return tok

    def settle(self, slot, regions):
        key = 'd_' + slot
        tok = (key, self.dcount[key])
        for r in regions:
            self.last_w[r] = tok

    def final_wait(self, eng, regions):
        waits = self._deps(eng, regions, ())
        self.instrs[eng].append((waits, None, None))

    def emit(self):
        nc = self.nc
        with nc.Block() as block:
            decos = {'pe': block.tensor, 'act': block.scalar, 'dve': block.vector,
                     'pool': block.gpsimd, 'sp': block.sync}
            for ename in self.ENGS:
                def make(ename):
                    def body(e):
                        for waits, fn, inc in self.instrs[ename]:
                            for k, v in waits:
                                e.wait_ge(self.sems[k], v)
                            if fn is not None:
                                ins = fn(e)
                                ins.then_inc(self.sems[inc[0]], inc[1])
                    return body
                decos[ename](make(ename))

from contextlib import ExitStack
from concourse.bass_utils import run_bass_kernel_spmd

D = 2048; T = 2048; TT = 512; NCH = 16
DR = 1024; NPAIR = 8
DFF = 5632; NFF = 44
C0 = 0.6065306597126334
RMS_EPS = 1e-6; GN_EPS = 64e-5

CP_FIELDS = [('ccol', 16), ('bada', 96), ('n1g', 16), ('n2g', 16), ('nfg', 16), ('mu', 27),
             ('w0', 8), ('a0', 8), ('kk', 8), ('ka', 8), ('rk', 8), ('lnw', 8), ('lnb', 8),
             ('poolb', 8), ('pools', 8), ('convw', 264), ('convb', 88), ('ident', 128),
             ('onesblk', 128), ('onesblkm', 128), ('amask', 512), ('rmask', 512), ('invd', 64)]
CP_OFF = {}
_o = 0
for _n, _w in CP_FIELDS:
    CP_OFF[_n] = (_o, _w); _o += _w
CP_W = _o


class _Stop(Exception):
    pass


def build_nc(NT=4, stage=99):
    nc = bass.Bass("TRN2", target_bir_lowering=False)
    es = ExitStack()
    with es:
        P = Prog(nc, es)
        din = lambda n, s: nc.dram_tensor(n, s, F32, kind="ExternalInput").ap()
        xT = din("xT", [D, T]).rearrange("(c p) t -> p c t", p=128)
        cpack_d = din("cpack", [128, CP_W])
        wada_d = din("wada", [128, 96, 16, 128])
        win_d = din("win", [128, 35, 16, 128])
        w2a2_d = din("w2a2", [128, 2, 1024])
        g2_d = din("g2", [128, 2, 1024])
        poolw_d = din("poolw", [128, 4, 2, 256])
        wout_d = din("wout", [128, 16, 16, 128])
        wup_d = din("wup", [128, 88, 16, 128])
        wdn_d = din("wdn", [128, 16, 44, 128])
        outT = nc.dram_tensor("outT", [D, T], F32, kind="ExternalOutput").ap().rearrange("(c p) t -> p c t", p=128)

        sb = lambda n, s, d: es.enter_context(nc.sbuf_tensor(n, s, d))
        psb = lambda n: es.enter_context(nc.psum_tensor(n, [128, 512], F32))

        cp = sb("cp", [128, CP_W], F32)
        def cv(name, a=None, b=None):
            o, w = CP_OFF[name]
            if a is None:
                return cp[:, o:o + w]
            return cp[:, o + a:o + (a + 1 if b is None else b)]
        xs = sb("xs", [128, 16, 512], F32)
        h = sb("h", [128, 16, 512], BF16)
        mix = sb("mix", [128, 16, 512], BF16)
        modsb = sb("modsb", [128, 96], F32)
        gs1 = sb("gs1", [128, 16], F32); gs2 = sb("gs2", [128, 16], F32)
        omka = sb("omka", [128, 8], F32)
        cact = sb("cact", [128, 16], BF16)
        ones_bf = sb("ones_bf", [128, 128], BF16)
        epsc = sb("epsc", [128, 2], F32)
        w2a2 = sb("w2a2s", [128, 2, 1024], BF16)
        g2s = sb("g2s", [128, 2, 1024], BF16)
        poolw = sb("poolws", [128, 4, 2, 256], BF16)
        Hst = sb("Hst", [128, 8, 128], F32)
        carry = sb("carry", [128, 27], F32)
        carryp = sb("carryp", [128, 8, 16], F32)
        carryf = sb("carryf", [128, 88, 2], F32)
        NSLOT = 3
        slots = [sb("wslot%d" % i, [128, 6144], BF16) for i in range(NSLOT)]
        ARENA = 20800
        arena = sb("arena", [128, ARENA], F32)
        banks = {n: psb(n) for n in ['acc0', 'acc1', 'st', 'pA', 'pD', 'pT', 'pW', 'pY']}

        class Arena:
            def __init__(self): self.off = 0
            def reset(self): self.off = 0
            def f32(self, n):
                v = arena[:, self.off:self.off + n]; self.off += n
                assert self.off <= ARENA, self.off
                return v
            def bf16(self, n):
                assert n % 2 == 0
                v = arena[:, self.off:self.off + n // 2].bitcast(BF16); self.off += n // 2
                assert self.off <= ARENA, self.off
                return v
        AR = Arena()

        def MM(out, lhsT, rhs, start, stop, r, w):
            P.op('pe', lambda e: e.matmul(out, lhsT=lhsT, rhs=rhs, start=start, stop=stop), r, w)
        def TR(out, in_, r, w):
            idt = cv('ident')
            P.op('pe', lambda e: e.transpose(out, in_, idt), list(r) + ['cp'], w)
        def ACT(out, in_, func, r, w, bias=None, scale=None):
            kw = {}
            if bias is not None: kw['bias'] = bias
            if scale is not None: kw['scale'] = scale
            P.op('act', lambda e: e.activation(out, in_, func, **kw), r, w)
        def TTo(eng, out, in0, in1, op, r, w):
            P.op(eng, lambda e: e.tensor_tensor(out, in0, in1, op), r, w)
        def TS(eng, out, in0, s1, s2, op0, op1, r, w):
            if s2 is None:
                P.op(eng, lambda e: e.tensor_scalar(out, in0, s1, None, op0), r, w)
            else:
                P.op(eng, lambda e: e.tensor_scalar(out, in0, s1, s2, op0, op1), r, w)
        def STT(eng, out, in0, scalar, in1, op0, op1, r, w):
            P.op(eng, lambda e: e.scalar_tensor_tensor(out, in0, scalar, in1, op0, op1), r, w)
        def CPY(eng, out, in_, r, w):
            if eng == 'act':
                P.op('act', lambda e: e.copy(out, in_), r, w)
            else:
                P.op(eng, lambda e: e.tensor_copy(out, in_), r, w)
        def MEMSET(eng, ap, val, w):
            P.op(eng, lambda e: e.memset(ap, val), (), w)
        def v3(ap):
            return ap.rearrange("p (a b) -> p a b", b=64)

        plan = []
        _fl = lambda ap: ap.rearrange("p a k n -> p (a k n)")
        for g in range(32):
            plan.append((_fl(wada_d[:, 3 * g:3 * g + 3]), 6144))
        for _it in range(NT):
            plan.append((_fl(win_d[:, 0:3]), 6144))
            for c in range(8):
                plan.append((_fl(win_d[:, 3 + 3 * c:6 + 3 * c]), 6144))
            for g3 in range(3):
                nb = 3 if g3 < 2 else 2
                plan.append((_fl(win_d[:, 27 + 3 * g3:27 + 3 * g3 + nb]), nb * 2048))
            for g3 in range(6):
                nb = 3 if g3 < 5 else 1
                plan.append((_fl(wout_d[:, 3 * g3:3 * g3 + nb]), nb * 2048))
            for j in range(NFF):
                plan.append((_fl(wup_d[:, 2 * j:2 * j + 2]), 4096))
            for blk in range(16):
                plan.append((wdn_d[:, blk].rearrange("p k n -> p (k n)"), NFF * 128))
        st = {'cur': 0, 'issued': 0}
        PF = 2
        def wload(src, nelem):
            k = st['cur']; st['cur'] += 1
            while st['issued'] <= min(k + PF, len(plan) - 1):
                i = st['issued']; st['issued'] += 1
                psrc, pn = plan[i]
                key = 'ws%d' % (i % NSLOT)
                P.dma('pool', slots[i % NSLOT][:, 0:pn], psrc, key, reads=(), writes=[key])
            assert plan[k][1] == nelem, (k, plan[k][1], nelem)
            return slots[k % NSLOT][:, 0:nelem], 'ws%d' % (k % NSLOT)

        P.dma('sp', cp[:], cpack_d, 'c', writes=['cp'])
        P.dma('pool', w2a2[:], w2a2_d, 'cw', writes=['w2a2'])
        P.dma('pool', g2s[:], g2_d, 'cw', writes=['g2s'])
        P.dma('pool', poolw[:], poolw_d, 'cw', writes=['poolw'])
        P.settle('cw', ['w2a2', 'g2s', 'poolw'])
        MEMSET('pool', ones_bf[:], 1.0, ['ones_bf'])
        MEMSET('pool', epsc[:, 0:1], RMS_EPS, ['epsc'])
        MEMSET('pool', epsc[:, 1:2], GN_EPS, ['epsc'])
        MEMSET('pool', Hst[:], 0.0, ['Hst%d' % c for c in range(8)])
        MEMSET('pool', carry[:], 0.0, ['carry'])
        MEMSET('pool', carryp[:], 0.0, ['carryp'])
        MEMSET('pool', carryf[:], 0.0, ['carryf'])
        ACT(cact[:], cv('ccol'), AF.Silu, ['cp'], ['cact'])
        TS('dve', omka[:], cv('ka'), -1.0, 1.0, ALU.mult, ALU.add, ['cp'], ['omka'])
        modp = banks['st']
        for g in range(32):
            wv, wk = wload(wada_d[:, 3 * g:3 * g + 3].rearrange("p a k n -> p (a k n)"), 6144)
            wv4 = wv.rearrange("p (a k n) -> p a k n", a=3, k=16)
            for a in range(3):
                j = 3 * g + a
                for kc in range(16):
                    MM(modp[:, j:j + 1], wv4[:, a, kc, :], cact[:, kc:kc + 1], kc == 0, kc == 15,
                       [wk, 'cact'], ['st'])
        TTo('dve', modsb[:], modp[:, 0:96], cv('bada'), ALU.add, ['st', 'cp'], ['modsb'])
        sh1 = modsb[:, 0:16]; sc1 = modsb[:, 16:32]; gt1 = modsb[:, 32:48]
        sh2 = modsb[:, 48:64]; sc2 = modsb[:, 64:80]; gt2 = modsb[:, 80:96]
        STT('dve', gs1[:], sc1, 1.0, cv('n1g'), ALU.add, ALU.mult, ['modsb', 'cp'], ['gs1'])
        STT('dve', gs2[:], sc2, 1.0, cv('n2g'), ALU.add, ALU.mult, ['modsb', 'cp'], ['gs2'])

        def rms_stats(tmp_sq, rstd):
            ss = banks['st']
            for c in range(16):
                sq = tmp_sq[c % 2]
                ACT(sq, xs[:, c, :], AF.Square, ['xs%d' % c], ['A:sq%d' % (c % 2)])
                MM(ss[:], ones_bf[:], sq, c == 0, c == 15, ['ones_bf', 'A:sq%d' % (c % 2)], ['st'])
            ACT(rstd, ss[:], AF.Ln, ['st', 'epsc'], ['A:rstd'], bias=epsc[:, 0:1], scale=1.0 / D)
            ACT(rstd, rstd, AF.Exp, ['A:rstd'], ['A:rstd'], scale=-0.5)

        def norm_to_h(gs, sh, gkey):
            AR.reset()
            tmp_sq = [AR.bf16(512), AR.bf16(512)]
            rstd = AR.f32(512)
            tmp = [AR.f32(512), AR.f32(512)]
            rms_stats(tmp_sq, rstd)
            for c in range(16):
                t = tmp[c % 2]
                TTo('dve' if c % 2 == 0 else 'pool', t, xs[:, c, :], rstd, ALU.mult,
                    ['xs%d' % c, 'A:rstd'], ['A:nt%d' % (c % 2)])
                ACT(h[:, c, :], t, AF.Identity, ['A:nt%d' % (c % 2), gkey, 'modsb'], ['h%d' % c],
                    bias=sh[:, c:c + 1], scale=gs[:, c:c + 1])

        hkeys = ['h%d' % c for c in range(16)]
        mixkeys = ['mix%d' % c for c in range(16)]
        accs = ['acc0', 'acc1']
        accn = {'n': 0}
        def next_acc(names=accs):
            n = names[accn['n'] % len(names)]; accn['n'] += 1
            return n

        outkeys = []
        def ck(n):
            if stage == n:
                raise _Stop()
        def dump(ap, slot, w=512):
            P.dma('sp', outT[:, slot, 0:w], ap, 'dbg', reads=['cp', 'modsb', 'gs1'] + ['A:tA', 'A:d', 'A:kk', 'A:k', 'A:r', 'A:v', 'A:s', 'A:a', 'A:g'] * (slot >= 4), writes=['dbg%d' % slot])
            outkeys.append('dbg%d' % slot)
        try:
          ck(0)
          for it in range(NT):
              t0 = it * TT
              for q in range(4):
                  P.dma('sp', xs[:, 4 * q:4 * q + 4, :], xT[:, 4 * q:4 * q + 4, t0:t0 + TT], 'x',
                        writes=['xs%d' % c for c in range(4 * q, 4 * q + 4)])
              P.settle('x', ['xs%d' % c for c in range(16)])
              P.fence_arena()
              norm_to_h(gs1, sh1, 'gs1')
              ck(1)
              P.fence_arena()

              AR.reset()
              praw = AR.f32(528)
              txa = AR.bf16(512); sxg0 = AR.bf16(512); sxg1 = AR.bf16(512)
              r32 = AR.f32(512); k32 = AR.f32(512); v32 = AR.f32(512); a32 = AR.f32(512)
              s32 = AR.f32(512); kk32 = AR.f32(512); g32 = AR.f32(512)
              off_alias = AR.off
              cum = AR.f32(512); epos = AR.f32(512); eneg = AR.f32(512)
              rt = AR.f32(512); tA = AR.f32(512); tB = AR.f32(512)
              atz = AR.f32(1024).rearrange("p (a b) -> p a b", b=128)
              btz = AR.f32(1024).rearrange("p (a b) -> p a b", b=128)
              ktz = AR.f32(1024).rearrange("p (a b) -> p a b", b=128)
              vz = AR.f32(1024).rearrange("p (a b) -> p a b", b=128)
              NS = 8
              AsbS = [AR.f32(512) for _ in range(NS)]
              XS = [AR.f32(128) for _ in range(NS)]
              QPS = [AR.f32(256) for _ in range(NS)]
              TsbS = [AR.f32(384) for _ in range(2)]
              Wsb = AR.f32(128); Usb = AR.f32(128)
              dtl = AR.f32(512)
              for zt, nm in ((atz, 'A:atz'), (btz, 'A:btz'), (ktz, 'A:ktz'), (vz, 'A:vz')):
                  MEMSET('pool', zt, 0.0, [nm])

              def inproj_block(wv4, wk, b, fixed=None):
                  an = fixed if fixed else next_acc(); ps = banks[an]
                  for kc in range(16):
                      MM(ps[:], wv4[:, b, kc, :], h[:, kc, :], kc == 0, kc == 15, [wk, 'h%d' % kc], [an])
                  return an, ps

              def lerp_evac(an, ps, blk, dst, dkey):
                  CPY('act', praw[:, 1:513], ps[:], [an], ['A:praw'])
                  CPY('pool', praw[:, 0:1], carry[:, blk:blk + 1], ['carry'], ['A:praw'])
                  CPY('pool', carry[:, blk:blk + 1], praw[:, 512:513], ['A:praw'], ['carry'])
                  TTo('dve', dtl, praw[:, 0:512], praw[:, 1:513], ALU.subtract, ['A:praw'], ['A:dtl'])
                  STT('dve', dst, dtl, cv('mu', blk), praw[:, 1:513], ALU.mult, ALU.add,
                      ['A:dtl', 'A:praw', 'cp'], [dkey])

              wv, wk = wload(win_d[:, 0:3].rearrange("p a k n -> p (a k n)"), 6144)
              wv4 = wv.rearrange("p (a k n) -> p a k n", a=3, k=16)
              an, ps = inproj_block(wv4, wk, 0)
              lerp_evac(an, ps, 0, tA, 'A:tA')
              ACT(txa[0:64, :], tA[0:64, :], AF.Tanh, ['A:tA'], ['A:txa'])
              ACT(txa[64:128, :], tA[64:128, :], AF.Identity, ['A:tA'], ['A:txa'])
              an, ps = inproj_block(wv4, wk, 1)
              lerp_evac(an, ps, 1, tA, 'A:tA')
              ACT(sxg0, tA, AF.Sigmoid, ['A:tA'], ['A:sxg0'])
              an, ps = inproj_block(wv4, wk, 2)
              lerp_evac(an, ps, 2, tA, 'A:tA')
              ACT(sxg1, tA, AF.Sigmoid, ['A:tA'], ['A:sxg1'])
              ck(2)

              for c in range(NPAIR):
                  cs = slice(128 * c, 128 * c + 128)
                  def chainI(cn, fixed='acc0'):
                      wvn, wkn = wload(win_d[:, 3 + 3 * cn:6 + 3 * cn].rearrange("p a k n -> p (a k n)"), 6144)
                      wvn4 = wvn.rearrange("p (a k n) -> p a k n", a=3, k=16)
                      for b, (dst, dk) in enumerate(((r32, 'A:r'), (k32, 'A:k'), (v32, 'A:v'))):
                          an, ps = inproj_block(wvn4, wkn, b, fixed=fixed)
                          lerp_evac(an, ps, 3 + 3 * cn + b, dst, dk)
                          yield
                  if True:
                      for _ in chainI(c, None):
                          pass
                  stb = banks['st']
                  rmask = cv('rmask')
                  for hh in range(2):
                      ph = slice(64 * hh, 64 * hh + 64)
                      CPY('pool', vz[ph, :, ph], v3(v32)[ph], ['A:v'], ['A:vz'])
                  MM(stb[:], w2a2[:, 0, cs], txa, True, True, ['w2a2', 'A:txa'], ['st'])
                  ACT(s32, stb[:], AF.Sigmoid, ['st', 'cp'], ['A:s'], bias=cv('w0', c))
                  P.op('dve', lambda e: e.tensor_tensor_scan(cum, rmask, s32, 0.0, ALU.mult, ALU.add),
                       ['cp', 'A:s'], ['A:cum'])
                  MM(stb[:], w2a2[:, 1, cs], txa, True, True, ['w2a2', 'A:txa'], ['st'])
                  ACT(a32, stb[:], AF.Sigmoid, ['st', 'cp'], ['A:a'], bias=cv('a0', c))
                  ACT(kk32, k32, AF.Copy, ['A:k', 'cp'], ['A:kk'], scale=cv('kk', c))
                  ACT(tA, kk32, AF.Square, ['A:kk'], ['A:tA'])
                  MM(stb[:], cv('onesblk'), tA, True, True, ['cp', 'A:tA'], ['st'])
                  ACT(tB, stb[:], AF.Sqrt, ['st'], ['A:tB'])
                  TS('dve', tA, a32, cv('ka', c), omka[:, c:c + 1], ALU.mult, ALU.add, ['A:a', 'cp', 'omka'], ['A:tA'])
                  TTo('dve', k32, k32, tA, ALU.mult, ['A:k', 'A:tA'], ['A:k'])
                  ACT(epos, cum, AF.Exp, ['A:cum'], ['A:epos'], scale=-C0)
                  ACT(eneg, cum, AF.Exp, ['A:cum'], ['A:eneg'], scale=C0)
                  TTo('pool', tA, cum, s32, ALU.subtract, ['A:cum', 'A:s'], ['A:tA'])
                  ACT(tA, tA, AF.Exp, ['A:tA'], ['A:tA'], scale=-C0)
                  TTo('dve', rt, r32, epos, ALU.mult, ['A:r', 'A:epos'], ['A:rt'])
                  TS('dve', tB, tB, 1e-12, None, ALU.max, None, ['A:tB'], ['A:tB'])
                  P.op('dve', lambda e: e.reciprocal(tB, tB), ['A:tB'], ['A:tB'])
                  TTo('dve', kk32, kk32, tB, ALU.mult, ['A:kk', 'A:tB'], ['A:kk'])
                  for hh in range(2):
                      ph = slice(64 * hh, 64 * hh + 64)
                      STT('dve', atz[ph, :, ph], v3(kk32)[ph], -1.0, v3(tA)[ph], ALU.mult, ALU.mult,
                          ['A:kk', 'A:tA'], ['A:atz'])
                      TTo('pool', ktz[ph, :, ph], v3(k32)[ph], v3(eneg)[ph], ALU.mult, ['A:k', 'A:eneg'], ['A:ktz'])
                  TTo('dve', tB, kk32, eneg, ALU.mult, ['A:kk', 'A:eneg'], ['A:tB'])
                  for hh in range(2):
                      ph = slice(64 * hh, 64 * hh + 64)
                      eng = 'dve' if hh == 0 else 'pool'
                      TTo(eng, btz[ph, :, ph], v3(tB)[ph], v3(a32)[ph], ALU.mult, ['A:tB', 'A:a'], ['A:btz'])
                  MM(stb[:], g2s[:, 0, cs], sxg0, True, False, ['g2s', 'A:sxg0'], ['st'])
                  MM(stb[:], g2s[:, 1, cs], sxg1, False, True, ['g2s', 'A:sxg1'], ['st'])
                  CPY('act', g32, stb[:], ['st'], ['A:g'])
                  ck(3)
                  Hbd = Hst[:, c, :]
                  hk = 'Hst%d' % c
                  pA = banks['pA']; pT = banks['pT']; pW = banks['pW']; pY = banks['pY']
                  _STAG = 3; _HOIST = 0; _NB = 4
                  dbanks = ['pD', 'st', 'acc1', 'acc0'][:_NB]
                  ddone = [False] * 8; sdone = [False] * 8

                  def chainD(j):
                      sl = j % NS
                      Asb = AsbS[sl]; Xs_ = XS[sl]; QPs_ = QPS[sl]
                      kA = 'A:Asb%d' % sl; kX = 'A:X%d' % sl; kQ = 'A:QP%d' % sl
                      bn = dbanks[sl % len(dbanks)]
                      pD = banks[bn]
                      ts_ = slice(64 * j, 64 * j + 64)
                      MM(pA[:, 0:128], btz[:, j, :], atz[:, j, :], True, True, ['A:btz', 'A:atz'], ['pA'])
                      MM(pA[:, 128:256], ktz[:, j, :], atz[:, j, :], True, True, ['A:ktz', 'A:atz'], ['pA'])
                      MM(pA[:, 256:384], atz[:, j, :], btz[:, j, :], True, True, ['A:btz', 'A:atz'], ['pA'])
                      MM(pA[:, 384:448], btz[:, j, :], rt[:, ts_], True, True, ['A:btz', 'A:rt'], ['pA'])
                      MM(pA[:, 448:512], ktz[:, j, :], rt[:, ts_], True, True, ['A:ktz', 'A:rt'], ['pA'])
                      TTo('dve', Asb, pA[:], cv('amask'), ALU.mult, ['pA', 'cp'], [kA])
                      TTo('pool', Xs_, Asb[:, 0:128], cv('ident'), ALU.add, [kA, 'cp'], [kX])
                      yield
                      Qc = Asb[:, 0:128]; Pc = Asb[:, 256:384]; qk = kA
                      for stp in range(5):
                          MM(pD[:, 0:128], Pc, Qc, True, True, [qk], [bn])
                          MM(pD[:, 128:256], Qc, Pc, True, True, [qk], [bn])
                          CPY('act', QPs_, pD[:, 0:256], [bn], [kQ])
                          yield
                          Qc = QPs_[:, 0:128]; Pc = QPs_[:, 128:256]; qk = kQ
                          MM(pD[:, 256:384], Pc, Xs_, True, True, [qk, kX], [bn])
                          TTo('dve', Xs_, Xs_, pD[:, 256:384], ALU.add, [bn, kX], [kX])
                          yield
                      ddone[j] = True
                      yield

                  tr_emitted = [False] * 9
                  def emit_tr(j):
                      if j >= 8 or tr_emitted[j]:
                          return
                      tr_emitted[j] = True
                      Tsb = TsbS[j % 2]; kT = 'A:Tsb%d' % (j % 2)
                      TR(pT[:, 0:128], vz[:, j, :], ['A:vz'], ['pT'])
                      TR(pT[:, 128:256], btz[:, j, :], ['A:btz'], ['pT'])
                      TR(pT[:, 256:384], ktz[:, j, :], ['A:ktz'], ['pT'])
                      CPY('act', Tsb, pT[:, 0:384], ['pT'], [kT])

                  def chainS():
                      for j in range(8):
                          while not ddone[j]:
                              yield
                          emit_tr(j)
                          sl = j % NS
                          Asb = AsbS[sl]; Xf = XS[sl]; Tsb = TsbS[j % 2]
                          kA = 'A:Asb%d' % sl; xk = 'A:X%d' % sl; kT = 'A:Tsb%d' % (j % 2)
                          ts_ = slice(64 * j, 64 * j + 64)
                          Vz = Tsb[:, 0:128]; bT = Tsb[:, 128:256]; kT_ = Tsb[:, 256:384]
                          MM(pW[:, 0:128], atz[:, j, :], Hbd, True, False, ['A:atz', hk], ['pW'])
                          MM(pW[:, 0:128], Asb[:, 128:256], Vz, False, True, [kA, kT], ['pW'])
                          CPY('act', Wsb, pW[:, 0:128], ['pW'], ['A:Wsb'])
                          yield
                          MM(pW[:, 128:256], Xf, Wsb, True, True, [xk, 'A:Wsb'], ['pW'])
                          CPY('dve', Usb, pW[:, 128:256], ['pW'], ['A:Usb'])
                          yield
                          MM(pY[:, ts_], Hbd, rt[:, ts_], True, False, [hk, 'A:rt'], ['pY'])
                          MM(pY[:, ts_], Usb, Asb[:, 384:448], False, False, ['A:Usb', kA], ['pY'])
                          MM(pY[:, ts_], Vz, Asb[:, 448:512], False, True, [kT, kA], ['pY'])
                          MM(pW[:, 256:384], cv('ident'), Hbd, True, False, ['cp', hk], ['pW'])
                          MM(pW[:, 256:384], bT, Usb, False, False, [kT, 'A:Usb'], ['pW'])
                          MM(pW[:, 256:384], kT_, Vz, False, True, [kT], ['pW'])
                          TS('dve', Hbd, pW[:, 256:384], epos[:, 64 * j + 63:64 * j + 64], None, ALU.mult, None,
                             ['pW', 'A:epos'], [hk])
                          sdone[j] = True
                          emit_tr(j + 1)
                          yield

                  STT('dve', tA, r32, cv('rk', c), k32, ALU.mult, ALU.mult, ['A:r', 'A:k', 'cp'], ['A:tA'])
                  MM(stb[:], cv('onesblk'), tA, True, True, ['cp', 'A:tA'], ['st'])
                  TTo('dve', tB, stb[:], v32, ALU.mult, ['st', 'A:v'], ['A:tB'])

                  chains = {}
                  nextD = 0
                  sgen = chainS(); s_alive = True
                  rnd = 0
                  i_started = (c == NPAIR - 1)
                  while s_alive or chains or nextD < 8:
                      if nextD < 8 and rnd >= _STAG * nextD:
                          chains[nextD] = chainD(nextD); nextD += 1
                      if _HOIST and not i_started and rnd >= _HOIST:
                          chains['I'] = chainI(c + 1); i_started = True
                      if s_alive:
                          try:
                              next(sgen)
                          except StopIteration:
                              s_alive = False
                      for jj in list(chains):
                          try:
                              next(chains[jj])
                          except StopIteration:
                              del chains[jj]
                      rnd += 1
                  ck(4)
                  CPY('act', tA, pY[:], ['pY'], ['A:tA'])
                  MM(stb[:], cv('onesblkm'), tA, True, True, ['cp', 'A:tA'], ['st'])
                  TTo('dve', cum, tA, stb[:], ALU.subtract, ['A:tA', 'st'], ['A:cum'])
                  ACT(tA, cum, AF.Square, ['A:cum'], ['A:tA'])
                  MM(stb[:], cv('onesblkm'), tA, True, True, ['cp', 'A:tA'], ['st'])
                  ACT(tA, stb[:], AF.Ln, ['st', 'epsc'], ['A:tA'], bias=epsc[:, 1:2])
                  ACT(tA, tA, AF.Exp, ['A:tA'], ['A:tA'], scale=-0.5)
                  TTo('dve', cum, cum, tA, ALU.mult, ['A:cum', 'A:tA'], ['A:cum'])
                  ACT(cum, cum, AF.Identity, ['A:cum', 'cp'], ['A:cum'], bias=cv('lnb', c), scale=cv('lnw', c))
                  TTo('pool', cum, cum, tB, ALU.add, ['A:cum', 'A:tB'], ['A:cum'])
                  TTo('dve', mix[:, c, :], cum, g32, ALU.mult, ['A:cum', 'A:g'], ['mix%d' % c])
                  ck(5)

              P.fence_arena()
              AR.off = off_alias
              ubuf = AR.f32(528); sA = AR.f32(528); sB = AR.f32(528)
              zbuf = AR.bf16(1024).rearrange("p (a b) -> p a b", b=512)
              for g3 in range(3):
                  nb = 3 if g3 < 2 else 2
                  wv, wk = wload(win_d[:, 27 + 3 * g3:27 + 3 * g3 + nb].rearrange("p a k n -> p (a k n)"), nb * 2048)
                  wv4 = wv.rearrange("p (a k n) -> p a k n", a=nb, k=16)
                  for b in range(nb):
                      c = 3 * g3 + b
                      gi = c // 2; m = gi + 1; win = 2 << gi
                      an, ps = inproj_block(wv4, wk, b)
                      CPY('act', ubuf[:, 16:528], ps[:], [an], ['A:ubuf'])
                      CPY('pool', ubuf[:, 0:16], carryp[:, c, :], ['carryp'], ['A:ubuf'])
                      CPY('pool', carryp[:, c, :], ubuf[:, 512:528], ['A:ubuf'], ['carryp'])
                      src = ubuf; sk = 'A:ubuf'
                      bufs = [(sA, 'A:sA'), (sB, 'A:sB')]
                      lo = 0
                      for s_ in range(m):
                          sh_ = 1 << s_
                          dst, dk = bufs[s_ % 2]
                          nlo = lo + sh_
                          TTo('pool' if s_ % 2 else 'dve', dst[:, nlo:528], src[:, nlo:528], src[:, nlo - sh_:528 - sh_],
                              ALU.add, [sk], [dk])
                          src, sk, lo = dst, dk, nlo
                      zc = zbuf[:, c % 2, :]
                      STT('dve', zc, src[:, 16:528], 1.0 / win, ubuf[:, 16:528], ALU.mult, ALU.subtract,
                          [sk, 'A:ubuf'], ['A:z%d' % (c % 2)])
                      if it == 0:
                          o_, _ = CP_OFF['invd']
                          ivd = cp[:, o_ + 16 * gi:o_ + 16 * gi + 16]
                          TTo('dve', tB[:, 0:16], src[:, 16:32], ivd, ALU.mult, [sk, 'cp'], ['A:tB'])
                          TTo('dve', zc[:, 0:16], tB[:, 0:16], ubuf[:, 16:32], ALU.subtract,
                              ['A:tB', 'A:ubuf'], ['A:z%d' % (c % 2)])
                      if c % 2 == 1:
                          for oc in range(2):
                              an2 = next_acc(); ps2 = banks[an2]
                              for k2 in range(2):
                                  MM(ps2[:], poolw[:, gi, k2, 128 * oc:128 * oc + 128], zbuf[:, k2, :], k2 == 0, k2 == 1,
                                     ['poolw', 'A:z%d' % k2], [an2])
                              cc = 2 * gi + oc
                              TS('dve', mix[:, 8 + cc, :], ps2[:], cv('poolb', cc), cv('pools', cc), ALU.add, ALU.mult,
                                 [an2, 'cp'], ['mix%d' % (8 + cc)])

              ck(6)
              for g3 in range(6):
                  nb = 3 if g3 < 5 else 1
                  wv, wk = wload(wout_d[:, 3 * g3:3 * g3 + nb].rearrange("p a k n -> p (a k n)"), nb * 2048)
                  wv4 = wv.rearrange("p (a k n) -> p a k n", a=nb, k=16)
                  for b in range(nb):
                      blk = 3 * g3 + b
                      an = next_acc(); ps = banks[an]
                      for kc in range(16):
                          MM(ps[:], wv4[:, b, kc, :], mix[:, kc, :], kc == 0, kc == 15, [wk, 'mix%d' % kc], [an])
                      STT('dve', xs[:, blk, :], ps[:], gt1[:, blk:blk + 1], xs[:, blk, :], ALU.mult, ALU.add,
                          [an, 'modsb', 'xs%d' % blk], ['xs%d' % blk])

              ck(7)
              P.fence_arena()
              norm_to_h(gs2, sh2, 'gs2')
              P.fence_arena()
              AR.reset()
              gated = AR.bf16(NFF * 512).rearrange("p (a b) -> p a b", b=512)
              ub = [AR.f32(514) for _ in range(4)]
              accb = [AR.f32(512) for _ in range(4)]
              facc = ['acc0', 'acc1', 'pA', 'pD']
              cwo, _ = CP_OFF['convw']; cbo, _ = CP_OFF['convb']
              for j in range(NFF):
                  wv, wk = wload(wup_d[:, 2 * j:2 * j + 2].rearrange("p a k n -> p (a k n)"), 4096)
                  wv4 = wv.rearrange("p (a k n) -> p a k n", a=2, k=16)
                  bi = [(2 * j) % 4, (2 * j + 1) % 4]
                  for b in range(2):
                      blk = 2 * j + b
                      an = next_acc(facc); ps = banks[an]
                      for kc in range(16):
                          MM(ps[:], wv4[:, b, kc, :], h[:, kc, :], kc == 0, kc == 15, [wk, 'h%d' % kc], [an])
                      u = ub[bi[b]]; uk = 'A:ub%d' % bi[b]; ac = accb[bi[b]]; ak = 'A:acc%d' % bi[b]
                      w_ = lambda jj: cp[:, cwo + 3 * blk + jj:cwo + 3 * blk + jj + 1]
                      CPY('act', u[:, 2:514], ps[:], [an], [uk])
                      CPY('pool', u[:, 0:2], carryf[:, blk, :], ['carryf'], [uk])
                      CPY('pool', carryf[:, blk, :], u[:, 512:514], [uk], ['carryf'])
                      TS('pool', ac, u[:, 2:514], w_(2), cp[:, cbo + blk:cbo + blk + 1], ALU.mult, ALU.add,
                         [uk, 'cp'], [ak])
                      STT('dve', ac, u[:, 1:513], w_(1), ac, ALU.mult, ALU.add, [uk, ak, 'cp'], [ak])
                      STT('dve', ac, u[:, 0:512], w_(0), ac, ALU.mult, ALU.add, [uk, ak, 'cp'], [ak])
                  ACT(accb[bi[1]], accb[bi[1]], AF.Silu, ['A:acc%d' % bi[1]], ['A:acc%d' % bi[1]])
                  TTo('dve', gated[:, j, :], accb[bi[0]], accb[bi[1]], ALU.mult, ['A:acc%d' % bi[0], 'A:acc%d' % bi[1]],
                      ['A:gated%d' % j])
              ck(8)
              for blk in range(16):
                  wv, wk = wload(wdn_d[:, blk].rearrange("p k n -> p (k n)"), NFF * 128)
                  wv3 = wv.rearrange("p (k n) -> p k n", k=NFF)
                  an = next_acc(facc); ps = banks[an]
                  for kc in range(NFF):
                      MM(ps[:], wv3[:, kc, :], gated[:, kc, :], kc == 0, kc == NFF - 1, [wk, 'A:gated%d' % kc], [an])
                  STT('dve', xs[:, blk, :], ps[:], gt2[:, blk:blk + 1], xs[:, blk, :], ALU.mult, ALU.add,
                      [an, 'modsb', 'xs%d' % blk], ['xs%d' % blk])

              ck(9)
              P.fence_arena()
              AR.reset()
              tmp_sq = [AR.bf16(512), AR.bf16(512)]
              rstd = AR.f32(512)
              tmp = [AR.f32(512), AR.f32(512)]
              ob = [AR.f32(512), AR.f32(512)]
              rms_stats(tmp_sq, rstd)
              for c in range(16):
                  t = tmp[c % 2]; o = ob[c % 2]
                  STT('dve', o, xs[:, c, :], cv('nfg', c), rstd, ALU.mult, ALU.mult, ['xs%d' % c, 'A:rstd', 'cp'],
                      ['A:ob%d' % (c % 2)])
                  P.dma('sp', outT[:, c, t0:t0 + TT], o, 'o%d' % (c % 2), reads=['A:ob%d' % (c % 2)],
                        writes=['out_%d_%d' % (it, c)])
                  outkeys.append('out_%d_%d' % (it, c))
        except _Stop:
            dump(modsb[:], 0, 96)
            dump(gs1[:], 1, 16)
        P.final_wait('sp', outkeys)
        P.emit()
    return nc


def _col(v, n):
    return np.ascontiguousarray(np.asarray(v, np.float32).reshape(n, 128).T)


def _prep_shared(inp):
    f = lambda k: np.asarray(inp[k], np.float32)
    cpk = np.zeros((128, CP_W), np.float32)
    def put(name, arr):
        o, w = CP_OFF[name]
        assert arr.shape == (128, w), (name, arr.shape, w)
        cpk[:, o:o + w] = arr
    put('bada', _col(f('b_ada')[0], 96))
    put('n1g', _col(f('norm1_g')[0], 16)); put('n2g', _col(f('norm2_g')[0], 16)); put('nfg', _col(f('norm_f_g'), 16))
    cols = np.full((35, 128), -1, np.int64)
    cols[0] = np.arange(3072, 3200); cols[1] = np.arange(3200, 3328); cols[2, :32] = np.arange(3328, 3360)
    for c in range(8):
        cols[3 + 3 * c] = np.arange(128 * c, 128 * c + 128)
        cols[4 + 3 * c] = 1024 + np.arange(128 * c, 128 * c + 128)
        cols[5 + 3 * c] = 2048 + np.arange(128 * c, 128 * c + 128)
        cols[27 + c] = 3360 + np.arange(128 * c, 128 * c + 128)
    flat = cols.reshape(-1)
    valid = flat >= 0
    w_in = f('w_in')[0]
    W = np.zeros((2048, 35 * 128), np.float32)
    W[:, valid] = w_in[:, flat[valid]]
    win = np.ascontiguousarray(W.reshape(16, 128, 35, 128).transpose(1, 2, 0, 3))
    mu = f('mu_shift')[0]
    mul = np.zeros((27 * 128,), np.float32)
    v27 = valid[:27 * 128]
    mul[v27] = mu[flat[:27 * 128][v27]]
    put('mu', _col(mul, 27))
    for nm, key in (('w0', 'w0'), ('a0', 'a0'), ('kk', 'k_k'), ('ka', 'k_a'), ('lnw', 'ln_x_w'), ('lnb', 'ln_x_b'),
                    ('pools', 'pool_scale')):
        put(nm, _col(f(key)[0], 8))
    put('rk', _col(f('r_k')[0].reshape(-1), 8))
    put('poolb', _col(f('pool_b')[0].reshape(-1), 8))
    perm = np.concatenate([np.concatenate([np.arange(128 * j, 128 * j + 128), DFF + np.arange(128 * j, 128 * j + 128)])
                           for j in range(NFF)])
    cw = f('conv_w')[0][:, perm]
    put('convw', np.ascontiguousarray(cw.reshape(3, 88, 128).transpose(2, 1, 0).reshape(128, 264)))
    put('convb', _col(f('conv_b')[0][perm], 88))
    put('ident', np.eye(128, dtype=np.float32))
    blk = np.kron(np.eye(2, dtype=np.float32), np.ones((64, 64), np.float32))
    put('onesblk', blk); put('onesblkm', blk / 64.0)
    su = np.triu(np.ones((64, 64), np.float32), 1)
    ui = np.triu(np.ones((64, 64), np.float32), 0)
    bd = lambda m: np.kron(np.eye(2, dtype=np.float32), m)
    st2 = lambda m: np.concatenate([m, m], axis=0)
    put('amask', np.concatenate([bd(su), bd(su), bd(su.T), st2(ui), st2(ui)], axis=1))
    rm = np.ones((128, 512), np.float32); rm[:, ::64] = 0.0
    put('rmask', rm)
    ivd = np.zeros((128, 64), np.float32)
    for gi in range(4):
        ivd[:, 16 * gi:16 * gi + 16] = 1.0 / np.minimum(np.arange(1, 17), 2 << gi)
    put('invd', ivd)
    sh = {}
    sh['wada'] = np.ascontiguousarray(f('w_ada')[0].reshape(16, 128, 96, 128).transpose(1, 2, 0, 3))
    sh['win'] = win
    wa = np.zeros((128, 2, 1024), np.float32)
    wa[:64, 0] = f('w2')[0]; wa[64:, 1] = f('a2')[0]
    sh['w2a2'] = wa
    g2 = f('g2')[0]
    g2l = np.zeros((128, 2, 1024), np.float32); g2l[:, 0] = g2[:128]; g2l[:32, 1] = g2[128:160]
    sh['g2'] = g2l
    sh['poolw'] = np.ascontiguousarray(f('pool_w')[0].reshape(4, 2, 128, 256).transpose(2, 0, 1, 3))
    sh['wout'] = np.ascontiguousarray(f('w_out')[0].reshape(16, 128, 16, 128).transpose(1, 2, 0, 3))
    sh['wup'] = np.ascontiguousarray(f('w_up')[0][:, perm].reshape(16, 128, 88, 128).transpose(1, 2, 0, 3))
    sh['wdn'] = np.ascontiguousarray(f('w_down')[0].reshape(44, 128, 16, 128).transpose(1, 2, 0, 3))
    return cpk, sh


def kernel(_NT=4, _cores=8, _stage=99, **inputs):
    x = np.asarray(inputs['x'], np.float32)
    c = np.asarray(inputs['c'], np.float32)
    cpk, sh = _prep_shared(inputs)
    ntok = _NT * TT
    in_maps = []
    for b in range(_cores):
        m = dict(sh)
        cb = cpk.copy()
        o, w = CP_OFF['ccol']
        cb[:, o:o + w] = _col(c[b], 16)
        m['cpack'] = cb
        m['xT'] = np.ascontiguousarray(x[b].T)
        in_maps.append(m)
    nc = build_nc(_NT, _stage)
    res = run_bass_kernel_spmd(nc, in_maps, core_ids=list(range(_cores)))
    out = np.stack([np.ascontiguousarray(res.results[b]['outT'].T) for b in range(_cores)], axis=0)
    return out[:, :ntok] if _NT < 4 else out
```

```python
import numpy as np
import concourse.bass as bass
import concourse.mybir as mybir

F32 = mybir.dt.float32
BF16 = mybir.dt.bfloat16
AF = mybir.ActivationFunctionType
ALU = mybir.AluOpType


class Prog:
    ENGS = ['pe', 'act', 'dve', 'pool', 'sp']

    def __init__(self, nc, es):
        self.nc = nc
        self.es = es
        self.sems = {}
        self.cnt = {e: 0 for e in ['pe', 'act', 'dve', 'pool']}
        self.instrs = {e: [] for e in self.ENGS}
        self.known = {e: {} for e in self.ENGS}
        self.last_w = {}
        self.readers = {}
        self.dcount = {}
        self.fence = {}

    def fence_arena(self):
        f = {}
        for d in (self.last_w, self.readers):
            for key in [k for k in d if k.startswith('A:')]:
                v = d.pop(key)
                items = [v] if isinstance(v, tuple) else list(v.items())
                for k2, v2 in items:
                    if f.get(k2, 0) < v2:
                        f[k2] = v2
        self.fence = f

    def sem(self, key):
        if key not in self.sems:
            self.sems[key] = self.es.enter_context(self.nc.semaphore(key))
        return self.sems[key]

    def _deps(self, eng, reads, writes):
        deps = {}

        def add(tok):
            if tok is None:
                return
            k, v = tok
            if deps.get(k, 0) < v:
                deps[k] = v
        for r in reads:
            add(self.last_w.get(r))
        if any(w.startswith('A:') for w in writes):
            for k, v in self.fence.items():
                add((k, v))
        for w in writes:
            add(self.last_w.get(w))
            for k, v in self.readers.get(w, {}).items():
                add((k, v))
        waits = []
        for k, v in deps.items():
            if eng == 'pe' and k == 'c_pe':
                continue
            if self.known[eng].get(k, 0) >= v:
                continue
            self.known[eng][k] = v
            waits.append((k, v))
        return waits

    def _commit(self, tok, reads, writes):
        for r in reads:
            d = self.readers.setdefault(r, {})
            if d.get(tok[0], 0) < tok[1]:
                d[tok[0]] = tok[1]
        for w in writes:
            self.last_w[w] = tok
            self.readers[w] = {}

    def op(self, eng, fn, reads=(), writes=()):
        waits = self._deps(eng, reads, writes)
        self.cnt[eng] += 1
        tok = ('c_' + eng, self.cnt[eng])
        self.sem(tok[0])
        self.instrs[eng].append((waits, fn, (tok[0], 1)))
        self._commit(tok, reads, writes)
        return tok

    def dma(self, q, out, in_, slot, reads=(), writes=(), **kw):
        waits = self._deps(q, reads, writes)
        key = 'd_' + slot
        self.sem(key)
        self.dcount[key] = self.dcount.get(key, 0) + 16
        tok = (key, self.dcount[key])
        self.instrs[q].append((waits, lambda e: e.dma_start(out=out, in_=in_, **kw), (key, 16)))
        self._commit(tok, reads, writes)
        return tok

    def settle(self, slot, regions):
        key = 'd_' + slot
        tok = (key, self.dcount[key])
        for r in regions:
            self.last_w[r] = tok

    def final_wait(self, eng, regions):
        waits = self._deps(eng, regions, ())
        self.instrs[eng].append((waits, None, None))

    def emit(self):
        nc = self.nc
        with nc.Block() as block:
            decos = {'pe': block.tensor, 'act': block.scalar, 'dve': block.vector,
                     'pool': block.gpsimd, 'sp': block.sync}
            for ename in self.ENGS:
                def make(ename):
                    def body(e):
                        for waits, fn, inc in self.instrs[ename]:
                            for k, v in waits:
                                e.wait_ge(self.sems[k], v)
                            if fn is not None:
                                ins = fn(e)
                                ins.then_inc(self.sems[inc[0]], inc[1])
                    return body
                decos[ename](make(ename))

from contextlib import ExitStack
from concourse.bass_utils import run_bass_kernel_spmd

D = 2048; T = 2048; TT = 512; NCH = 16
DR = 1024; NPAIR = 8
DFF = 5632; NFF = 44
C0 = 0.6065306597126334
RMS_EPS = 1e-6; GN_EPS = 64e-5

CP_FIELDS = [('ccol', 16), ('bada', 96), ('n1g', 16), ('n2g', 16), ('nfg', 16), ('mu', 27),
             ('w0', 8), ('a0', 8), ('kk', 8), ('ka', 8), ('rk', 8), ('lnw', 8), ('lnb', 8),
             ('poolb', 8), ('pools', 8), ('convw', 264), ('convb', 88), ('ident', 128),
             ('onesblk', 128), ('onesblkm', 128), ('amask', 512), ('rmask', 512), ('invd', 64)]
CP_OFF = {}
_o = 0
for _n, _w in CP_FIELDS:
    CP_OFF[_n] = (_o, _w); _o += _w
CP_W = _o


class _Stop(Exception):
    pass


def build_nc(NT=4, stage=99):
    nc = bass.Bass("TRN2", target_bir_lowering=False)
    es = ExitStack()
    with es:
        P = Prog(nc, es)
        din = lambda n, s: nc.dram_tensor(n, s, F32, kind="ExternalInput").ap()
        xT = din("xT", [D, T]).rearrange("(c p) t -> p c t", p=128)
        cpack_d = din("cpack", [128, CP_W])
        wada_d = din("wada", [128, 96, 16, 128])
        win_d = din("win", [128, 35, 16, 128])
        w2a2_d = din("w2a2", [128, 2, 1024])
        g2_d = din("g2", [128, 2, 1024])
        poolw_d = din("poolw", [128, 4, 2, 256])
        wout_d = din("wout", [128, 16, 16, 128])
        wup_d = din("wup", [128, 88, 16, 128])
        wdn_d = din("wdn", [128, 16, 44, 128])
        outT = nc.dram_tensor("outT", [D, T], F32, kind="ExternalOutput").ap().rearrange("(c p) t -> p c t", p=128)

        sb = lambda n, s, d: es.enter_context(nc.sbuf_tensor(n, s, d))
        psb = lambda n: es.enter_context(nc.psum_tensor(n, [128, 512], F32))

        cp = sb("cp", [128, CP_W], F32)
        def cv(name, a=None, b=None):
            o, w = CP_OFF[name]
            if a is None:
                return cp[:, o:o + w]
            return cp[:, o + a:o + (a + 1 if b is None else b)]
        xs = sb("xs", [128, 16, 512], F32)
        h = sb("h", [128, 16, 512], BF16)
        mix = sb("mix", [128, 16, 512], BF16)
        modsb = sb("modsb", [128, 96], F32)
        gs1 = sb("gs1", [128, 16], F32); gs2 = sb("gs2", [128, 16], F32)
        omka = sb("omka", [128, 8], F32)
        cact = sb("cact", [128, 16], BF16)
        ones_bf = sb("ones_bf", [128, 128], BF16)
        epsc = sb("epsc", [128, 3], F32)
        w2a2 = sb("w2a2s", [128, 2, 1024], BF16)
        g2s = sb("g2s", [128, 2, 1024], BF16)
        poolw = sb("poolws", [128, 4, 2, 256], BF16)
        Hst = sb("Hst", [128, 8, 128], F32)
        carry = sb("carry", [128, 27], F32)
        carryp = sb("carryp", [128, 8, 16], F32)
        carryf = sb("carryf", [128, 88, 2], F32)
        NSLOT = 3
        slots = [sb("wslot%d" % i, [128, 6144], BF16) for i in range(NSLOT)]
        ARENA = 20800
        arena = sb("arena", [128, ARENA], F32)
        banks = {n: psb(n) for n in ['acc0', 'acc1', 'st', 'pA', 'pD', 'pT', 'pW', 'pY']}

        class Arena:
            def __init__(self): self.off = 0
            def reset(self): self.off = 0
            def f32(self, n):
                v = arena[:, self.off:self.off + n]; self.off += n
                assert self.off <= ARENA, self.off
                return v
            def bf16(self, n):
                assert n % 2 == 0
                v = arena[:, self.off:self.off + n // 2].bitcast(BF16); self.off += n // 2
                assert self.off <= ARENA, self.off
                return v
        AR = Arena()

        def MM(out, lhsT, rhs, start, stop, r, w):
            P.op('pe', lambda e: e.matmul(out, lhsT=lhsT, rhs=rhs, start=start, stop=stop), r, w)
        def TR(out, in_, r, w):
            idt = cv('ident')
            P.op('pe', lambda e: e.transpose(out, in_, idt), list(r) + ['cp'], w)
        def ACT(out, in_, func, r, w, bias=None, scale=None):
            kw = {}
            if bias is not None: kw['bias'] = bias
            if scale is not None: kw['scale'] = scale
            P.op('act', lambda e: e.activation(out, in_, func, **kw), r, w)
        def TTo(eng, out, in0, in1, op, r, w):
            P.op(eng, lambda e: e.tensor_tensor(out, in0, in1, op), r, w)
        def TS(eng, out, in0, s1, s2, op0, op1, r, w):
            if s2 is None:
                P.op(eng, lambda e: e.tensor_scalar(out, in0, s1, None, op0), r, w)
            else:
                P.op(eng, lambda e: e.tensor_scalar(out, in0, s1, s2, op0, op1), r, w)
        def STT(eng, out, in0, scalar, in1, op0, op1, r, w):
            P.op(eng, lambda e: e.scalar_tensor_tensor(out, in0, scalar, in1, op0, op1), r, w)
        def CPY(eng, out, in_, r, w):
            if eng == 'act':
                P.op('act', lambda e: e.copy(out, in_), r, w)
            else:
                P.op(eng, lambda e: e.tensor_copy(out, in_), r, w)
        def MEMSET(eng, ap, val, w):
            P.op(eng, lambda e: e.memset(ap, val), (), w)
        def v3(ap):
            return ap.rearrange("p (a b) -> p a b", b=64)

        plan = []
        _fl = lambda ap: ap.rearrange("p a k n -> p (a k n)")
        for g in range(32):
            plan.append((_fl(wada_d[:, 3 * g:3 * g + 3]), 6144))
        for _it in range(NT):
            plan.append((_fl(win_d[:, 0:3]), 6144))
            for c in range(8):
                plan.append((_fl(win_d[:, 3 + 3 * c:6 + 3 * c]), 6144))
            for g3 in range(3):
                nb = 3 if g3 < 2 else 2
                plan.append((_fl(win_d[:, 27 + 3 * g3:27 + 3 * g3 + nb]), nb * 2048))
            for g3 in range(6):
                nb = 3 if g3 < 5 else 1
                plan.append((_fl(wout_d[:, 3 * g3:3 * g3 + nb]), nb * 2048))
            for j in range(NFF):
                plan.append((_fl(wup_d[:, 2 * j:2 * j + 2]), 4096))
            for blk in range(16):
                plan.append((wdn_d[:, blk].rearrange("p k n -> p (k n)"), NFF * 128))
        st = {'cur': 0, 'issued': 0}
        PF = 2
        def wload(src, nelem):
            k = st['cur']; st['cur'] += 1
            while st['issued'] <= min(k + PF, len(plan) - 1):
                i = st['issued']; st['issued'] += 1
                psrc, pn = plan[i]
                key = 'ws%d' % (i % NSLOT)
                P.dma('pool', slots[i % NSLOT][:, 0:pn], psrc, key, reads=(), writes=[key])
            assert plan[k][1] == nelem, (k, plan[k][1], nelem)
            return slots[k % NSLOT][:, 0:nelem], 'ws%d' % (k % NSLOT)

        P.dma('sp', cp[:], cpack_d, 'c', writes=['cp'])
        P.dma('pool', w2a2[:], w2a2_d, 'cw', writes=['w2a2'])
        P.dma('pool', g2s[:], g2_d, 'cw', writes=['g2s'])
        P.dma('pool', poolw[:], poolw_d, 'cw', writes=['poolw'])
        P.settle('cw', ['w2a2', 'g2s', 'poolw'])
        MEMSET('pool', ones_bf[:], 1.0, ['ones_bf'])
        MEMSET('pool', epsc[:, 0:1], RMS_EPS, ['epsc'])
        MEMSET('pool', epsc[:, 1:2], GN_EPS, ['epsc'])
        MEMSET('pool', epsc[:, 2:3], 10.39720770839918, ['epsc'])
        MEMSET('pool', Hst[:], 0.0, ['Hst%d' % c for c in range(8)])
        MEMSET('pool', carry[:], 0.0, ['carry'])
        MEMSET('pool', carryp[:], 0.0, ['carryp'])
        MEMSET('pool', carryf[:], 0.0, ['carryf'])
        ACT(cact[:], cv('ccol'), AF.Silu, ['cp'], ['cact'])
        TS('dve', omka[:], cv('ka'), -1.0, 1.0, ALU.mult, ALU.add, ['cp'], ['omka'])
        modp = banks['st']
        for g in range(32):
            wv, wk = wload(wada_d[:, 3 * g:3 * g + 3].rearrange("p a k n -> p (a k n)"), 6144)
            wv4 = wv.rearrange("p (a k n) -> p a k n", a=3, k=16)
            for a in range(3):
                j = 3 * g + a
                for kc in range(16):
                    MM(modp[:, j:j + 1], wv4[:, a, kc, :], cact[:, kc:kc + 1], kc == 0, kc == 15,
                       [wk, 'cact'], ['st'])
        TTo('dve', modsb[:], modp[:, 0:96], cv('bada'), ALU.add, ['st', 'cp'], ['modsb'])
        sh1 = modsb[:, 0:16]; sc1 = modsb[:, 16:32]; gt1 = modsb[:, 32:48]
        sh2 = modsb[:, 48:64]; sc2 = modsb[:, 64:80]; gt2 = modsb[:, 80:96]
        STT('dve', gs1[:], sc1, 1.0, cv('n1g'), ALU.add, ALU.mult, ['modsb', 'cp'], ['gs1'])
        STT('dve', gs2[:], sc2, 1.0, cv('n2g'), ALU.add, ALU.mult, ['modsb', 'cp'], ['gs2'])

        def rms_stats(tmp_sq, rstd):
            ss = banks['st']
            for c in range(16):
                sq = tmp_sq[c % 2]
                ACT(sq, xs[:, c, :], AF.Square, ['xs%d' % c], ['A:sq%d' % (c % 2)])
                MM(ss[:], ones_bf[:], sq, c == 0, c == 15, ['ones_bf', 'A:sq%d' % (c % 2)], ['st'])
            ACT(rstd, ss[:], AF.Ln, ['st', 'epsc'], ['A:rstd'], bias=epsc[:, 0:1], scale=1.0 / D)
            ACT(rstd, rstd, AF.Exp, ['A:rstd'], ['A:rstd'], scale=-0.5)

        def norm_to_h(gs, sh, gkey):
            AR.reset()
            tmp_sq = [AR.bf16(512), AR.bf16(512)]
            rstd = AR.f32(512)
            tmp = [AR.f32(512), AR.f32(512)]
            rms_stats(tmp_sq, rstd)
            for c in range(16):
                t = tmp[c % 2]
                TTo('dve' if c % 2 == 0 else 'pool', t, xs[:, c, :], rstd, ALU.mult,
                    ['xs%d' % c, 'A:rstd'], ['A:nt%d' % (c % 2)])
                ACT(h[:, c, :], t, AF.Identity, ['A:nt%d' % (c % 2), gkey, 'modsb'], ['h%d' % c],
                    bias=sh[:, c:c + 1], scale=gs[:, c:c + 1])

        hkeys = ['h%d' % c for c in range(16)]
        mixkeys = ['mix%d' % c for c in range(16)]
        accs = ['acc0', 'acc1']
        accn = {'n': 0}
        def next_acc(names=accs):
            n = names[accn['n'] % len(names)]; accn['n'] += 1
            return n

        outkeys = []
        def ck(n):
            if stage == n:
                raise _Stop()
        def dump(ap, slot, w=512):
            P.dma('sp', outT[:, slot, 0:w], ap, 'dbg', reads=['cp', 'modsb', 'gs1'] + ['A:tA', 'A:d', 'A:kk', 'A:k', 'A:r', 'A:v', 'A:s', 'A:a', 'A:g'] * (slot >= 4), writes=['dbg%d' % slot])
            outkeys.append('dbg%d' % slot)
        try:
          ck(0)
          for it in range(NT):
              t0 = it * TT
              for q in range(4):
                  P.dma('sp', xs[:, 4 * q:4 * q + 4, :], xT[:, 4 * q:4 * q + 4, t0:t0 + TT], 'x',
                        writes=['xs%d' % c for c in range(4 * q, 4 * q + 4)])
              P.settle('x', ['xs%d' % c for c in range(16)])
              P.fence_arena()
              norm_to_h(gs1, sh1, 'gs1')
              ck(1)
              P.fence_arena()

              AR.reset()
              praw = AR.f32(528)
              txa = AR.bf16(512); sxg0 = AR.bf16(512); sxg1 = AR.bf16(512)
              r32 = AR.f32(512); k32 = AR.f32(512); v32 = AR.f32(512); a32 = AR.f32(512)
              s32 = AR.f32(512); kk32 = AR.f32(512); g32 = AR.f32(512)
              off_alias = AR.off
              cum = AR.f32(512); epos = AR.f32(512); eneg = AR.f32(512)
              rt = AR.f32(512); tA = AR.f32(512); tB = AR.f32(512)
              atz = AR.f32(1024).rearrange("p (a b) -> p a b", b=128)
              btz = AR.f32(1024).rearrange("p (a b) -> p a b", b=128)
              ktz = AR.f32(1024).rearrange("p (a b) -> p a b", b=128)
              vz = AR.f32(1024).rearrange("p (a b) -> p a b", b=128)
              NS = 8
              AsbS = [AR.f32(512) for _ in range(NS)]
              XS = [AR.f32(128) for _ in range(NS)]
              QPS = [AR.f32(256) for _ in range(NS)]
              TsbS = [AR.f32(384) for _ in range(2)]
              Wsb = AR.f32(128); Usb = AR.f32(128)
              dtl = AR.f32(512)
              for zt, nm in ((atz, 'A:atz'), (btz, 'A:btz'), (ktz, 'A:ktz'), (vz, 'A:vz')):
                  MEMSET('pool', zt, 0.0, [nm])

              def inproj_block(wv4, wk, b, fixed=None):
                  an = fixed if fixed else next_acc(); ps = banks[an]
                  for kc in range(16):
                      MM(ps[:], wv4[:, b, kc, :], h[:, kc, :], kc == 0, kc == 15, [wk, 'h%d' % kc], [an])
                  return an, ps

              def lerp_evac(an, ps, blk, dst, dkey):
                  CPY('act', praw[:, 1:513], ps[:], [an], ['A:praw'])
                  CPY('pool', praw[:, 0:1], carry[:, blk:blk + 1], ['carry'], ['A:praw'])
                  CPY('pool', carry[:, blk:blk + 1], praw[:, 512:513], ['A:praw'], ['carry'])
                  TTo('dve', dtl, praw[:, 0:512], praw[:, 1:513], ALU.subtract, ['A:praw'], ['A:dtl'])
                  STT('dve', dst, dtl, cv('mu', blk), praw[:, 1:513], ALU.mult, ALU.add,
                      ['A:dtl', 'A:praw', 'cp'], [dkey])

              wv, wk = wload(win_d[:, 0:3].rearrange("p a k n -> p (a k n)"), 6144)
              wv4 = wv.rearrange("p (a k n) -> p a k n", a=3, k=16)
              an, ps = inproj_block(wv4, wk, 0)
              lerp_evac(an, ps, 0, tA, 'A:tA')
              ACT(txa[0:64, :], tA[0:64, :], AF.Tanh, ['A:tA'], ['A:txa'])
              ACT(txa[64:128, :], tA[64:128, :], AF.Identity, ['A:tA'], ['A:txa'])
              an, ps = inproj_block(wv4, wk, 1)
              lerp_evac(an, ps, 1, tA, 'A:tA')
              ACT(sxg0, tA, AF.Sigmoid, ['A:tA'], ['A:sxg0'])
              an, ps = inproj_block(wv4, wk, 2)
              lerp_evac(an, ps, 2, tA, 'A:tA')
              ACT(sxg1, tA, AF.Sigmoid, ['A:tA'], ['A:sxg1'])
              ck(2)

              for c in range(NPAIR):
                  cs = slice(128 * c, 128 * c + 128)
                  def chainI(cn, fixed='acc0'):
                      wvn, wkn = wload(win_d[:, 3 + 3 * cn:6 + 3 * cn].rearrange("p a k n -> p (a k n)"), 6144)
                      wvn4 = wvn.rearrange("p (a k n) -> p a k n", a=3, k=16)
                      for b, (dst, dk) in enumerate(((r32, 'A:r'), (k32, 'A:k'), (v32, 'A:v'))):
                          an, ps = inproj_block(wvn4, wkn, b, fixed=fixed)
                          lerp_evac(an, ps, 3 + 3 * cn + b, dst, dk)
                          yield
                  if True:
                      for _ in chainI(c, None):
                          pass
                  stb = banks['st']
                  rmask = cv('rmask')
                  for hh in range(2):
                      ph = slice(64 * hh, 64 * hh + 64)
                      CPY('pool', vz[ph, :, ph], v3(v32)[ph], ['A:v'], ['A:vz'])
                  MM(stb[:], w2a2[:, 0, cs], txa, True, True, ['w2a2', 'A:txa'], ['st'])
                  ACT(s32, stb[:], AF.Sigmoid, ['st', 'cp'], ['A:s'], bias=cv('w0', c))
                  P.op('dve', lambda e: e.tensor_tensor_scan(cum, rmask, s32, 0.0, ALU.mult, ALU.add),
                       ['cp', 'A:s'], ['A:cum'])
                  MM(stb[:], w2a2[:, 1, cs], txa, True, True, ['w2a2', 'A:txa'], ['st'])
                  ACT(a32, stb[:], AF.Sigmoid, ['st', 'cp'], ['A:a'], bias=cv('a0', c))
                  ACT(kk32, k32, AF.Copy, ['A:k', 'cp'], ['A:kk'], scale=cv('kk', c))
                  ACT(tA, kk32, AF.Square, ['A:kk'], ['A:tA'])
                  MM(stb[:], cv('onesblk'), tA, True, True, ['cp', 'A:tA'], ['st'])
                  TS('dve', tB, stb[:], 1e-24, None, ALU.max, None, ['st'], ['A:tB'])
                  ACT(tB, tB, AF.Ln, ['A:tB'], ['A:tB'], scale=1073741824.0)
                  ACT(tB, tB, AF.Exp, ['A:tB', 'epsc'], ['A:tB'], bias=epsc[:, 2:3], scale=-0.5)
                  TS('dve', tA, a32, cv('ka', c), omka[:, c:c + 1], ALU.mult, ALU.add, ['A:a', 'cp', 'omka'], ['A:tA'])
                  TTo('dve', k32, k32, tA, ALU.mult, ['A:k', 'A:tA'], ['A:k'])
                  ACT(epos, cum, AF.Exp, ['A:cum'], ['A:epos'], scale=-C0)
                  ACT(eneg, cum, AF.Exp, ['A:cum'], ['A:eneg'], scale=C0)
                  TTo('pool', tA, cum, s32, ALU.subtract, ['A:cum', 'A:s'], ['A:tA'])
                  ACT(tA, tA, AF.Exp, ['A:tA'], ['A:tA'], scale=-C0)
                  TTo('dve', rt, r32, epos, ALU.mult, ['A:r', 'A:epos'], ['A:rt'])
                  TTo('dve', kk32, kk32, tB, ALU.mult, ['A:kk', 'A:tB'], ['A:kk'])
                  for hh in range(2):
                      ph = slice(64 * hh, 64 * hh + 64)
                      STT('dve', atz[ph, :, ph], v3(kk32)[ph], -1.0, v3(tA)[ph], ALU.mult, ALU.mult,
                          ['A:kk', 'A:tA'], ['A:atz'])
                      TTo('pool', ktz[ph, :, ph], v3(k32)[ph], v3(eneg)[ph], ALU.mult, ['A:k', 'A:eneg'], ['A:ktz'])
                  TTo('dve', tB, kk32, eneg, ALU.mult, ['A:kk', 'A:eneg'], ['A:tB'])
                  for hh in range(2):
                      ph = slice(64 * hh, 64 * hh + 64)
                      eng = 'dve' if hh == 0 else 'pool'
                      TTo(eng, btz[ph, :, ph], v3(tB)[ph], v3(a32)[ph], ALU.mult, ['A:tB', 'A:a'], ['A:btz'])
                  MM(stb[:], g2s[:, 0, cs], sxg0, True, False, ['g2s', 'A:sxg0'], ['st'])
                  MM(stb[:], g2s[:, 1, cs], sxg1, False, True, ['g2s', 'A:sxg1'], ['st'])
                  CPY('act', g32, stb[:], ['st'], ['A:g'])
                  ck(3)
                  Hbd = Hst[:, c, :]
                  hk = 'Hst%d' % c
                  pA = banks['pA']; pT = banks['pT']; pW = banks['pW']; pY = banks['pY']
                  _STAG = 3; _HOIST = 0; _NB = 4
                  dbanks = ['pD', 'st', 'acc1', 'acc0'][:_NB]
                  ddone = [False] * 8; sdone = [False] * 8

                  def chainD(j):
                      sl = j % NS
                      Asb = AsbS[sl]; Xs_ = XS[sl]; QPs_ = QPS[sl]
                      kA = 'A:Asb%d' % sl; kX = 'A:X%d' % sl; kQ = 'A:QP%d' % sl
                      bn = dbanks[sl % len(dbanks)]
                      pD = banks[bn]
                      ts_ = slice(64 * j, 64 * j + 64)
                      MM(pA[:, 0:128], btz[:, j, :], atz[:, j, :], True, True, ['A:btz', 'A:atz'], ['pA'])
                      MM(pA[:, 128:256], ktz[:, j, :], atz[:, j, :], True, True, ['A:ktz', 'A:atz'], ['pA'])
                      MM(pA[:, 256:384], atz[:, j, :], btz[:, j, :], True, True, ['A:btz', 'A:atz'], ['pA'])
                      MM(pA[:, 384:448], btz[:, j, :], rt[:, ts_], True, True, ['A:btz', 'A:rt'], ['pA'])
                      MM(pA[:, 448:512], ktz[:, j, :], rt[:, ts_], True, True, ['A:ktz', 'A:rt'], ['pA'])
                      TTo('dve', Asb, pA[:], cv('amask'), ALU.mult, ['pA', 'cp'], [kA])
                      TTo('pool', Xs_, Asb[:, 0:128], cv('ident'), ALU.add, [kA, 'cp'], [kX])
                      yield
                      Qc = Asb[:, 0:128]; Pc = Asb[:, 256:384]; qk = kA
                      for stp in range(5):
                          MM(pD[:, 0:128], Pc, Qc, True, True, [qk], [bn])
                          MM(pD[:, 128:256], Qc, Pc, True, True, [qk], [bn])
                          CPY('act', QPs_, pD[:, 0:256], [bn], [kQ])
                          yield
                          Qc = QPs_[:, 0:128]; Pc = QPs_[:, 128:256]; qk = kQ
                          MM(pD[:, 256:384], Pc, Xs_, True, True, [qk, kX], [bn])
                          TTo('dve', Xs_, Xs_, pD[:, 256:384], ALU.add, [bn, kX], [kX])
                          yield
                      ddone[j] = True
                      yield

                  tr_emitted = [False] * 9
                  def emit_tr(j):
                      if j >= 8 or tr_emitted[j]:
                          return
                      tr_emitted[j] = True
                      Tsb = TsbS[j % 2]; kT = 'A:Tsb%d' % (j % 2)
                      TR(pT[:, 0:128], vz[:, j, :], ['A:vz'], ['pT'])
                      TR(pT[:, 128:256], btz[:, j, :], ['A:btz'], ['pT'])
                      TR(pT[:, 256:384], ktz[:, j, :], ['A:ktz'], ['pT'])
                      CPY('act', Tsb, pT[:, 0:384], ['pT'], [kT])

                  def chainS():
                      for j in range(8):
                          while not ddone[j]:
                              yield
                          emit_tr(j)
                          sl = j % NS
                          Asb = AsbS[sl]; Xf = XS[sl]; Tsb = TsbS[j % 2]
                          kA = 'A:Asb%d' % sl; xk = 'A:X%d' % sl; kT = 'A:Tsb%d' % (j % 2)
                          ts_ = slice(64 * j, 64 * j + 64)
                          Vz = Tsb[:, 0:128]; bT = Tsb[:, 128:256]; kT_ = Tsb[:, 256:384]
                          MM(pW[:, 0:128], atz[:, j, :], Hbd, True, False, ['A:atz', hk], ['pW'])
                          MM(pW[:, 0:128], Asb[:, 128:256], Vz, False, True, [kA, kT], ['pW'])
                          CPY('act', Wsb, pW[:, 0:128], ['pW'], ['A:Wsb'])
                          yield
                          MM(pW[:, 128:256], Xf, Wsb, True, True, [xk, 'A:Wsb'], ['pW'])
                          CPY('dve', Usb, pW[:, 128:256], ['pW'], ['A:Usb'])
                          yield
                          MM(pY[:, ts_], Hbd, rt[:, ts_], True, False, [hk, 'A:rt'], ['pY'])
                          MM(pY[:, ts_], Usb, Asb[:, 384:448], False, False, ['A:Usb', kA], ['pY'])
                          MM(pY[:, ts_], Vz, Asb[:, 448:512], False, True, [kT, kA], ['pY'])
                          MM(pW[:, 256:384], cv('ident'), Hbd, True, False, ['cp', hk], ['pW'])
                          MM(pW[:, 256:384], bT, Usb, False, False, [kT, 'A:Usb'], ['pW'])
                          MM(pW[:, 256:384], kT_, Vz, False, True, [kT], ['pW'])
                          TS('dve', Hbd, pW[:, 256:384], epos[:, 64 * j + 63:64 * j + 64], None, ALU.mult, None,
                             ['pW', 'A:epos'], [hk])
                          sdone[j] = True
                          emit_tr(j + 1)
                          yield

                  STT('dve', tA, r32, cv('rk', c), k32, ALU.mult, ALU.mult, ['A:r', 'A:k', 'cp'], ['A:tA'])
                  MM(stb[:], cv('onesblk'), tA, True, True, ['cp', 'A:tA'], ['st'])
                  TTo('dve', tB, stb[:], v32, ALU.mult, ['st', 'A:v'], ['A:tB'])

                  chains = {}
                  nextD = 0
                  sgen = chainS(); s_alive = True
                  rnd = 0
                  i_started = (c == NPAIR - 1)
                  while s_alive or chains or nextD < 8:
                      if nextD < 8 and rnd >= _STAG * nextD:
                          chains[nextD] = chainD(nextD); nextD += 1
                      if _HOIST and not i_started and rnd >= _HOIST:
                          chains['I'] = chainI(c + 1); i_started = True
                      if s_alive:
                          try:
                              next(sgen)
                          except StopIteration:
                              s_alive = False
                      for jj in list(chains):
                          try:
                              next(chains[jj])
                          except StopIteration:
                              del chains[jj]
                      rnd += 1
                  ck(4)
                  CPY('act', tA, pY[:], ['pY'], ['A:tA'])
                  MM(stb[:], cv('onesblkm'), tA, True, True, ['cp', 'A:tA'], ['st'])
                  TTo('dve', cum, tA, stb[:], ALU.subtract, ['A:tA', 'st'], ['A:cum'])
                  ACT(tA, cum, AF.Square, ['A:cum'], ['A:tA'])
                  MM(stb[:], cv('onesblkm'), tA, True, True, ['cp', 'A:tA'], ['st'])
                  ACT(tA, stb[:], AF.Ln, ['st', 'epsc'], ['A:tA'], bias=epsc[:, 1:2])
                  ACT(tA, tA, AF.Exp, ['A:tA'], ['A:tA'], scale=-0.5)
                  TTo('dve', cum, cum, tA, ALU.mult, ['A:cum', 'A:tA'], ['A:cum'])
                  ACT(cum, cum, AF.Identity, ['A:cum', 'cp'], ['A:cum'], bias=cv('lnb', c), scale=cv('lnw', c))
                  TTo('pool', cum, cum, tB, ALU.add, ['A:cum', 'A:tB'], ['A:cum'])
                  TTo('dve', mix[:, c, :], cum, g32, ALU.mult, ['A:cum', 'A:g'], ['mix%d' % c])
                  ck(5)

              P.fence_arena()
              AR.off = off_alias
              ubuf = AR.f32(528); sA = AR.f32(528); sB = AR.f32(528)
              zbuf = AR.bf16(1024).rearrange("p (a b) -> p a b", b=512)
              for g3 in range(3):
                  nb = 3 if g3 < 2 else 2
                  wv, wk = wload(win_d[:, 27 + 3 * g3:27 + 3 * g3 + nb].rearrange("p a k n -> p (a k n)"), nb * 2048)
                  wv4 = wv.rearrange("p (a k n) -> p a k n", a=nb, k=16)
                  for b in range(nb):
                      c = 3 * g3 + b
                      gi = c // 2; m = gi + 1; win = 2 << gi
                      an, ps = inproj_block(wv4, wk, b)
                      CPY('act', ubuf[:, 16:528], ps[:], [an], ['A:ubuf'])
                      CPY('pool', ubuf[:, 0:16], carryp[:, c, :], ['carryp'], ['A:ubuf'])
                      CPY('pool', carryp[:, c, :], ubuf[:, 512:528], ['A:ubuf'], ['carryp'])
                      src = ubuf; sk = 'A:ubuf'
                      bufs = [(sA, 'A:sA'), (sB, 'A:sB')]
                      lo = 0
                      for s_ in range(m):
                          sh_ = 1 << s_
                          dst, dk = bufs[s_ % 2]
                          nlo = lo + sh_
                          TTo('pool' if s_ % 2 else 'dve', dst[:, nlo:528], src[:, nlo:528], src[:, nlo - sh_:528 - sh_],
                              ALU.add, [sk], [dk])
                          src, sk, lo = dst, dk, nlo
                      zc = zbuf[:, c % 2, :]
                      STT('dve', zc, src[:, 16:528], 1.0 / win, ubuf[:, 16:528], ALU.mult, ALU.subtract,
                          [sk, 'A:ubuf'], ['A:z%d' % (c % 2)])
                      if it == 0:
                          o_, _ = CP_OFF['invd']
                          ivd = cp[:, o_ + 16 * gi:o_ + 16 * gi + 16]
                          TTo('dve', tB[:, 0:16], src[:, 16:32], ivd, ALU.mult, [sk, 'cp'], ['A:tB'])
                          TTo('dve', zc[:, 0:16], tB[:, 0:16], ubuf[:, 16:32], ALU.subtract,
                              ['A:tB', 'A:ubuf'], ['A:z%d' % (c % 2)])
                      if c % 2 == 1:
                          for oc in range(2):
                              an2 = next_acc(); ps2 = banks[an2]
                              for k2 in range(2):
                                  MM(ps2[:], poolw[:, gi, k2, 128 * oc:128 * oc + 128], zbuf[:, k2, :], k2 == 0, k2 == 1,
                                     ['poolw', 'A:z%d' % k2], [an2])
                              cc = 2 * gi + oc
                              TS('dve', mix[:, 8 + cc, :], ps2[:], cv('poolb', cc), cv('pools', cc), ALU.add, ALU.mult,
                                 [an2, 'cp'], ['mix%d' % (8 + cc)])

              ck(6)
              for g3 in range(6):
                  nb = 3 if g3 < 5 else 1
                  wv, wk = wload(wout_d[:, 3 * g3:3 * g3 + nb].rearrange("p a k n -> p (a k n)"), nb * 2048)
                  wv4 = wv.rearrange("p (a k n) -> p a k n", a=nb, k=16)
                  for b in range(nb):
                      blk = 3 * g3 + b
                      an = next_acc(); ps = banks[an]
                      for kc in range(16):
                          MM(ps[:], wv4[:, b, kc, :], mix[:, kc, :], kc == 0, kc == 15, [wk, 'mix%d' % kc], [an])
                      STT('dve', xs[:, blk, :], ps[:], gt1[:, blk:blk + 1], xs[:, blk, :], ALU.mult, ALU.add,
                          [an, 'modsb', 'xs%d' % blk], ['xs%d' % blk])

              ck(7)
              P.fence_arena()
              norm_to_h(gs2, sh2, 'gs2')
              P.fence_arena()
              AR.reset()
              gated = AR.bf16(NFF * 512).rearrange("p (a b) -> p a b", b=512)
              ub = [AR.f32(514) for _ in range(4)]
              accb = [AR.f32(512) for _ in range(4)]
              facc = ['acc0', 'acc1', 'pA', 'pD']
              cwo, _ = CP_OFF['convw']; cbo, _ = CP_OFF['convb']
              for j in range(NFF):
                  wv, wk = wload(wup_d[:, 2 * j:2 * j + 2].rearrange("p a k n -> p (a k n)"), 4096)
                  wv4 = wv.rearrange("p (a k n) -> p a k n", a=2, k=16)
                  bi = [(2 * j) % 4, (2 * j + 1) % 4]
                  for b in range(2):
                      blk = 2 * j + b
                      an = next_acc(facc); ps = banks[an]
                      for kc in range(16):
                          MM(ps[:], wv4[:, b, kc, :], h[:, kc, :], kc == 0, kc == 15, [wk, 'h%d' % kc], [an])
                      u = ub[bi[b]]; uk = 'A:ub%d' % bi[b]; ac = accb[bi[b]]; ak = 'A:acc%d' % bi[b]
                      w_ = lambda jj: cp[:, cwo + 3 * blk + jj:cwo + 3 * blk + jj + 1]
                      CPY('act', u[:, 2:514], ps[:], [an], [uk])
                      CPY('pool', u[:, 0:2], carryf[:, blk, :], ['carryf'], [uk])
                      CPY('pool', carryf[:, blk, :], u[:, 512:514], [uk], ['carryf'])
                      TS('pool', ac, u[:, 2:514], w_(2), cp[:, cbo + blk:cbo + blk + 1], ALU.mult, ALU.add,
                         [uk, 'cp'], [ak])
                      STT('dve', ac, u[:, 1:513], w_(1), ac, ALU.mult, ALU.add, [uk, ak, 'cp'], [ak])
                      STT('dve', ac, u[:, 0:512], w_(0), ac, ALU.mult, ALU.add, [uk, ak, 'cp'], [ak])
                  ACT(accb[bi[1]], accb[bi[1]], AF.Silu, ['A:acc%d' % bi[1]], ['A:acc%d' % bi[1]])
                  TTo('dve', gated[:, j, :], accb[bi[0]], accb[bi[1]], ALU.mult, ['A:acc%d' % bi[0], 'A:acc%d' % bi[1]],
                      ['A:gated%d' % j])
              ck(8)
              for blk in range(16):
                  wv, wk = wload(wdn_d[:, blk].rearrange("p k n -> p (k n)"), NFF * 128)
                  wv3 = wv.rearrange("p (k n) -> p k n", k=NFF)
                  an = next_acc(facc); ps = banks[an]
                  for kc in range(NFF):
                      MM(ps[:], wv3[:, kc, :], gated[:, kc, :], kc == 0, kc == NFF - 1, [wk, 'A:gated%d' % kc], [an])
                  STT('dve', xs[:, blk, :], ps[:], gt2[:, blk:blk + 1], xs[:, blk, :], ALU.mult, ALU.add,
                      [an, 'modsb', 'xs%d' % blk], ['xs%d' % blk])

              ck(9)
              P.fence_arena()
              AR.reset()
              tmp_sq = [AR.bf16(512), AR.bf16(512)]
              rstd = AR.f32(512)
              tmp = [AR.f32(512), AR.f32(512)]
              ob = [AR.f32(512), AR.f32(512)]
              rms_stats(tmp_sq, rstd)
              for c in range(16):
                  t = tmp[c % 2]; o = ob[c % 2]
                  STT('dve', o, xs[:, c, :], cv('nfg', c), rstd, ALU.mult, ALU.mult, ['xs%d' % c, 'A:rstd', 'cp'],
                      ['A:ob%d' % (c % 2)])
                  P.dma('sp', outT[:, c, t0:t0 + TT], o, 'o%d' % (c % 2), reads=['A:ob%d' % (c % 2)],
                        writes=['out_%d_%d' % (it, c)])
                  outkeys.append('out_%d_%d' % (it, c))
        except _Stop:
            dump(modsb[:], 0, 96)
            dump(gs1[:], 1, 16)
        P.final_wait('sp', outkeys)
        P.emit()
    return nc


def _col(v, n):
    return np.ascontiguousarray(np.asarray(v, np.float32).reshape(n, 128).T)


def _prep_shared(inp):
    f = lambda k: np.asarray(inp[k], np.float32)
    cpk = np.zeros((128, CP_W), np.float32)
    def put(name, arr):
        o, w = CP_OFF[name]
        assert arr.shape == (128, w), (name, arr.shape, w)
        cpk[:, o:o + w] = arr
    put('bada', _col(f('b_ada')[0], 96))
    put('n1g', _col(f('norm1_g')[0], 16)); put('n2g', _col(f('norm2_g')[0], 16)); put('nfg', _col(f('norm_f_g'), 16))
    cols = np.full((35, 128), -1, np.int64)
    cols[0] = np.arange(3072, 3200); cols[1] = np.arange(3200, 3328); cols[2, :32] = np.arange(3328, 3360)
    for c in range(8):
        cols[3 + 3 * c] = np.arange(128 * c, 128 * c + 128)
        cols[4 + 3 * c] = 1024 + np.arange(128 * c, 128 * c + 128)
        cols[5 + 3 * c] = 2048 + np.arange(128 * c, 128 * c + 128)
        cols[27 + c] = 3360 + np.arange(128 * c, 128 * c + 128)
    flat = cols.reshape(-1)
    valid = flat >= 0
    w_in = f('w_in')[0]
    W = np.zeros((2048, 35 * 128), np.float32)
    W[:, valid] = w_in[:, flat[valid]]
    win = np.ascontiguousarray(W.reshape(16, 128, 35, 128).transpose(1, 2, 0, 3))
    mu = f('mu_shift')[0]
    mul = np.zeros((27 * 128,), np.float32)
    v27 = valid[:27 * 128]
    mul[v27] = mu[flat[:27 * 128][v27]]
    put('mu', _col(mul, 27))
    for nm, key in (('w0', 'w0'), ('a0', 'a0'), ('kk', 'k_k'), ('ka', 'k_a'), ('lnw', 'ln_x_w'), ('lnb', 'ln_x_b'),
                    ('pools', 'pool_scale')):
        put(nm, _col(f(key)[0], 8))
    put('rk', _col(f('r_k')[0].reshape(-1), 8))
    put('poolb', _col(f('pool_b')[0].reshape(-1), 8))
    perm = np.concatenate([np.concatenate([np.arange(128 * j, 128 * j + 128), DFF + np.arange(128 * j, 128 * j + 128)])
                           for j in range(NFF)])
    cw = f('conv_w')[0][:, perm]
    put('convw', np.ascontiguousarray(cw.reshape(3, 88, 128).transpose(2, 1, 0).reshape(128, 264)))
    put('convb', _col(f('conv_b')[0][perm], 88))
    put('ident', np.eye(128, dtype=np.float32))
    blk = np.kron(np.eye(2, dtype=np.float32), np.ones((64, 64), np.float32))
    put('onesblk', blk); put('onesblkm', blk / 64.0)
    su = np.triu(np.ones((64, 64), np.float32), 1)
    ui = np.triu(np.ones((64, 64), np.float32), 0)
    bd = lambda m: np.kron(np.eye(2, dtype=np.float32), m)
    st2 = lambda m: np.concatenate([m, m], axis=0)
    put('amask', np.concatenate([bd(su), bd(su), bd(su.T), st2(ui), st2(ui)], axis=1))
    rm = np.ones((128, 512), np.float32); rm[:, ::64] = 0.0
    put('rmask', rm)
    ivd = np.zeros((128, 64), np.float32)
    for gi in range(4):
        ivd[:, 16 * gi:16 * gi + 16] = 1.0 / np.minimum(np.arange(1, 17), 2 << gi)
    put('invd', ivd)
    sh = {}
    sh['wada'] = np.ascontiguousarray(f('w_ada')[0].reshape(16, 128, 96, 128).transpose(1, 2, 0, 3))
    sh['win'] = win
    wa = np.zeros((128, 2, 1024), np.float32)
    wa[:64, 0] = f('w2')[0]; wa[64:, 1] = f('a2')[0]
    sh['w2a2'] = wa
    g2 = f('g2')[0]
    g2l = np.zeros((128, 2, 1024), np.float32); g2l[:, 0] = g2[:128]; g2l[:32, 1] = g2[128:160]
    sh['g2'] = g2l
    sh['poolw'] = np.ascontiguousarray(f('pool_w')[0].reshape(4, 2, 128, 256).transpose(2, 0, 1, 3))
    sh['wout'] = np.ascontiguousarray(f('w_out')[0].reshape(16, 128, 16, 128).transpose(1, 2, 0, 3))
    sh['wup'] = np.ascontiguousarray(f('w_up')[0][:, perm].reshape(16, 128, 88, 128).transpose(1, 2, 0, 3))
    sh['wdn'] = np.ascontiguousarray(f('w_down')[0].reshape(44, 128, 16, 128).transpose(1, 2, 0, 3))
    return cpk, sh


def kernel(_NT=4, _cores=8, _stage=99, **inputs):
    x = np.asarray(inputs['x'], np.float32)
    c = np.asarray(inputs['c'], np.float32)
    cpk, sh = _prep_shared(inputs)
    ntok = _NT * TT
    in_maps = []
    for b in range(_cores):
        m = dict(sh)
        cb = cpk.copy()
        o, w = CP_OFF['ccol']
        cb[:, o:o + w] = _col(c[b], 16)
        m['cpack'] = cb
        m['xT'] = np.ascontiguousarray(x[b].T)
        in_maps.append(m)
    nc = build_nc(_NT, _stage)
    res = run_bass_kernel_spmd(nc, in_maps, core_ids=list(range(_cores)))
    out = np.stack([np.ascontiguousarray(res.results[b]['outT'].T) for b in range(_cores)], axis=0)
    return out[:, :ntok] if _NT < 4 else out
```
